# Optimizing a Trainium2 kernel written in Bass

```python
import jax, jax.numpy as jnp
from jax import lax
import numpy as np

D_MODEL = 1024
BATCH = 32
SEQ = 2048
DEPTH = 1

MLA_HEADS = 4
QK_NOPE_DIM = 128
QK_ROPE_DIM = 64
QK_DIM = QK_NOPE_DIM + QK_ROPE_DIM
V_HEAD_DIM = 128
Q_LORA_RANK = 512
KV_LORA_RANK = 256
MLA_WIDTH = MLA_HEADS * V_HEAD_DIM
ROPE_THETA = 10000.0
Q_BLOCK = 128
POOL_WINDOWS = (2, 4, 8, 16)
POOL_GROUPS = len(POOL_WINDOWS)
POOL_WIDTH = D_MODEL - MLA_WIDTH
POOL_GROUP_DIM = POOL_WIDTH // POOL_GROUPS
MIX_WIDTH = MLA_WIDTH + POOL_WIDTH
IN_SPLITS = (Q_LORA_RANK, KV_LORA_RANK, QK_ROPE_DIM, MLA_WIDTH, POOL_WIDTH, POOL_WIDTH)
IN_WIDTH = sum(IN_SPLITS)
RMS_EPS = 1e-6
LN_EPS = 1e-5

kernel_name = "hybrid_mla_multiscale_pool_deepnorm"


def rms_norm(x, g):
    xf = x.astype(jnp.float32)
    inv = lax.rsqrt(jnp.mean(xf * xf, axis=-1, keepdims=True) + RMS_EPS)
    return (xf * inv).astype(x.dtype) * g


def layer_norm(x, g, b):
    xf = x.astype(jnp.float32)
    mu = jnp.mean(xf, axis=-1, keepdims=True)
    var = jnp.mean(jnp.square(xf - mu), axis=-1, keepdims=True)
    return ((xf - mu) * lax.rsqrt(var + LN_EPS)).astype(x.dtype) * g + b


def rope_cos_sin(positions, dtype):
    half = QK_ROPE_DIM // 2
    inv_freq = ROPE_THETA ** (-jnp.arange(half, dtype=jnp.float32) / half)
    ang = positions.astype(jnp.float32)[..., None] * inv_freq
    return jnp.cos(ang).astype(dtype), jnp.sin(ang).astype(dtype)


def apply_rope(t, cos, sin):
    t1, t2 = jnp.split(t, 2, axis=-1)
    return jnp.concatenate([t1 * cos - t2 * sin, t1 * sin + t2 * cos], axis=-1)


def mla_branch(x_q, x_kv, k_rope_raw, positions, q_norm_g, w_uq, kv_norm_g, w_ukv):
    B, S, _ = x_q.shape
    cos, sin = rope_cos_sin(positions, x_q.dtype)
    q = (rms_norm(x_q, q_norm_g) @ w_uq).reshape(B, S, MLA_HEADS, QK_DIM)
    q_nope, q_rope = q[..., :QK_NOPE_DIM], q[..., QK_NOPE_DIM:]
    q_rope = apply_rope(q_rope, cos[:, :, None, :], sin[:, :, None, :])
    kv = (rms_norm(x_kv, kv_norm_g) @ w_ukv).reshape(B, S, MLA_HEADS, QK_NOPE_DIM + V_HEAD_DIM)
    k_nope, v = kv[..., :QK_NOPE_DIM], kv[..., QK_NOPE_DIM:]
    k_rope = apply_rope(k_rope_raw, cos, sin)
    scale = QK_DIM ** -0.5
    nb = S // Q_BLOCK
    qn_b = q_nope.reshape(B, nb, Q_BLOCK, MLA_HEADS, QK_NOPE_DIM).transpose(1, 0, 2, 3, 4)
    qr_b = q_rope.reshape(B, nb, Q_BLOCK, MLA_HEADS, QK_ROPE_DIM).transpose(1, 0, 2, 3, 4)
    pos_b = positions.reshape(B, nb, Q_BLOCK).transpose(1, 0, 2)
    neg = jnp.finfo(jnp.float32).min

    def attend(args):
        qn, qr, pq = args
        s = (jnp.einsum('bqhd,bkhd->bhqk', qn, k_nope)
             + jnp.einsum('bqhr,bkr->bhqk', qr, k_rope)).astype(jnp.float32) * scale
        mask = pq[:, None, :, None] >= positions[:, None, None, :]
        p = jax.nn.softmax(jnp.where(mask, s, neg), axis=-1)
        return jnp.einsum('bhqk,bkhd->bqhd', p.astype(v.dtype), v)

    o = lax.map(attend, (qn_b, qr_b, pos_b))
    return o.transpose(1, 0, 2, 3, 4).reshape(B, S, MLA_WIDTH)


def pool_branch(u, pool_w, pool_scale):
    B, S, _ = u.shape
    uf = u.astype(jnp.float32).reshape(B, S, POOL_GROUPS, POOL_GROUP_DIM)
    cs = jnp.concatenate([jnp.zeros((B, 1, POOL_GROUPS, POOL_GROUP_DIM), jnp.float32),
                          jnp.cumsum(uf, axis=1)], axis=1)
    hi = jnp.arange(S) + 1
    means = []
    for g, w in enumerate(POOL_WINDOWS):
        lo = jnp.maximum(hi - w, 0)
        cnt = (hi - lo).astype(jnp.float32)[None, :, None]
        means.append((cs[:, hi, g] - cs[:, lo, g]) / cnt)
    pooled = jnp.stack(means, axis=2) - uf
    mixed = jnp.einsum('bsgc,gcd->bsgd', pooled.astype(u.dtype), pool_w)
    return mixed.reshape(B, S, POOL_WIDTH) * pool_scale


def setup_inputs(seed: int = 0) -> dict:
    key = jax.random.key(seed)
    ks = jax.random.split(key, 12)
    beta = (8.0 * DEPTH) ** -0.25
    nrm = jax.random.normal
    return {
        "x": nrm(ks[0], (BATCH, SEQ, D_MODEL), jnp.float32),
        "positions": jnp.broadcast_to(jnp.arange(SEQ, dtype=jnp.int32), (BATCH, SEQ)),
        "w_in": nrm(ks[1], (D_MODEL, IN_WIDTH), jnp.float32) * D_MODEL ** -0.5,
        "q_norm_g": 1.0 + 0.05 * nrm(ks[2], (Q_LORA_RANK,), jnp.float32),
        "w_uq": nrm(ks[3], (Q_LORA_RANK, MLA_HEADS * QK_DIM), jnp.float32) * Q_LORA_RANK ** -0.5,
        "kv_norm_g": 1.0 + 0.05 * nrm(ks[4], (KV_LORA_RANK,), jnp.float32),
        "w_ukv": nrm(ks[5], (KV_LORA_RANK, MLA_HEADS * (QK_NOPE_DIM + V_HEAD_DIM)), jnp.float32) * KV_LORA_RANK ** -0.5,
        "pool_w": nrm(ks[6], (POOL_GROUPS, POOL_GROUP_DIM, POOL_GROUP_DIM), jnp.float32) * POOL_GROUP_DIM ** -0.5,
        "pool_scale": 1.0 + 0.1 * nrm(ks[7], (POOL_WIDTH,), jnp.float32),
        "w_out": nrm(ks[8], (MIX_WIDTH, D_MODEL), jnp.float32) * (MIX_WIDTH ** -0.5) * beta,
        "ln_g": 1.0 + 0.05 * nrm(ks[9], (DEPTH, D_MODEL), jnp.float32),
        "ln_b": 0.02 * nrm(ks[10], (DEPTH, D_MODEL), jnp.float32),
    }


def reference(x, positions, w_in, q_norm_g, w_uq, kv_norm_g, w_ukv, pool_w, pool_scale, w_out, ln_g, ln_b):
    alpha = (2.0 * DEPTH) ** 0.25
    splits = [int(c) for c in np.cumsum(IN_SPLITS)[:-1]]
    for layer in range(DEPTH):
        h = x @ w_in
        x_q, x_kv, k_rope_raw, gate_a, u_pool, gate_b = jnp.split(h, splits, axis=-1)
        y_a = mla_branch(x_q, x_kv, k_rope_raw, positions, q_norm_g, w_uq, kv_norm_g, w_ukv) * jax.nn.silu(gate_a)
        y_b = pool_branch(u_pool, pool_w, pool_scale) * jax.nn.silu(gate_b)
        mix = jnp.concatenate([y_a, y_b], axis=-1) @ w_out
        x = layer_norm(alpha * x + mix, ln_g[layer], ln_b[layer])
    return x
```

```python
import numpy as np
from contextlib import ExitStack
import concourse.bass as bass
import concourse.mybir as mybir
from concourse.bass_utils import run_bass_kernel_spmd

F32 = mybir.dt.float32
BF16 = mybir.dt.bfloat16
I32 = mybir.dt.int32
ALU = mybir.AluOpType
AF = mybir.ActivationFunctionType

NCORES = 8
BATCH = 32
S = 2048
D = 1024
NSEQ = BATCH // NCORES
TG = 512
NTG = S // TG
SCALE = 192.0 ** -0.5
ALPHA = 2.0 ** 0.25
RMS_EPS = 1e-6
LN_EPS = 1e-5
NEG = -30000.0
POOL_W = (2, 4, 8, 16)
HALO = 16
C_GA, C_GB, C_U, C_XQ, C_XKV, C_KR, C_ROT = 0, 512, 1024, 1536, 2048, 2304, 2368
WIN_COLS = 2432


class Tok:
    __slots__ = ("sem", "key", "val", "eng")

    def __init__(self, sem, key, val, eng):
        self.sem, self.key, self.val, self.eng = sem, key, val, eng


class Buf:
    __slots__ = ("name", "w", "r")

    def __init__(self, name):
        self.name, self.w, self.r = name, None, {}


class Eng:
    def __init__(self, name, h, sem, is_pe=False):
        self.name, self.h, self.sem, self.is_pe = name, h, sem, is_pe
        self.n = 0
        self.seen = {}


class DmaSem:
    def __init__(self, name, sem):
        self.name, self.sem, self.n = name, sem, 0


def _wait_deps(E, reads, writes):
    deps = []
    for b in reads:
        if b.w is not None:
            deps.append(b.w)
    for b in writes:
        if b.w is not None:
            deps.append(b.w)
        deps.extend(b.r.values())
    for t in deps:
        if t.eng is E and E.is_pe:
            continue
        if E.seen.get(t.key, 0) >= t.val:
            continue
        E.h.wait_ge(t.sem, t.val)
        E.seen[t.key] = t.val


def _record(tok, reads, writes):
    for b in reads:
        o = b.r.get(tok.key)
        if o is None or o.val < tok.val:
            b.r[tok.key] = tok
    for b in writes:
        b.w = tok
        b.r = {}


def op(E, fn, reads=(), writes=(), inc=True):
    _wait_deps(E, reads, writes)
    ins = fn()
    if inc:
        E.n += 1
        ins.then_inc(E.sem, 1)
        tok = Tok(E.sem, E.name, E.n, E)
    else:
        tok = Tok(E.sem, E.name, E.n + 1, E)
    _record(tok, reads, writes)
    return ins


def dma(Q, ds, out_ap, in_ap, reads=(), writes=(), **kw):
    _wait_deps(Q, reads, writes)
    ins = Q.h.dma_start(out=out_ap, in_=in_ap, **kw)
    ds.n += 16
    ins.then_inc(ds.sem, 16)
    tok = Tok(ds.sem, ds.name, ds.n, None)
    _record(tok, reads, writes)
    return ins


def build_nc(nseq=NSEQ):
    nc = bass.Bass("TRN2", target_bir_lowering=False)

    def din(name, shape, dt=F32):
        return nc.dram_tensor(name, list(shape), dt, kind="ExternalInput").ap()

    x = din("x", [nseq, S, D])
    positions = din("positions", [nseq, S], I32)
    w_in = din("w_in", [1024, 2368])
    q_norm_g = din("q_norm_g", [512])
    w_uq = din("w_uq", [512, 768])
    kv_norm_g = din("kv_norm_g", [256])
    w_ukv = din("w_ukv", [256, 1024])
    pool_w = din("pool_w", [4, 128, 128])
    pool_scale = din("pool_scale", [512])
    w_out = din("w_out", [1024, 1024])
    ln_g = din("ln_g", [1, 1024])
    ln_b = din("ln_b", [1, 1024])
    cst_cols = din("cst_cols", [128, 8])
    pool_c = din("pool_c", [128, 4 * HALO])
    cst_mats = din("cst_mats", [128, 3, 128])
    out = nc.dram_tensor("out", [nseq, S, D], F32, kind="ExternalOutput").ap()

    with ExitStack() as es:
        def sb(name, shape, dt):
            return es.enter_context(nc.sbuf_tensor(name, list(shape), dt))

        def newsem(name):
            return es.enter_context(nc.semaphore(name))

        PE = Eng("pe", nc.tensor, newsem("s_pe"), is_pe=True)
        ACT = Eng("act", nc.scalar, newsem("s_act"))
        DVE = Eng("dve", nc.vector, newsem("s_dve"))
        POOL = Eng("pool", nc.gpsimd, newsem("s_pool"))
        SP = Eng("sp", nc.sync, newsem("s_sp"))

        def dsem(name):
            return DmaSem(name, newsem("d_" + name))

        banks = [es.enter_context(nc.psum_tensor(f"pb{i}", [128, 512], F32)) for i in range(8)]
        bbuf = [Buf(f"bank{i}") for i in range(8)]

        w_in_bf = sb("w_in_bf", [128, 8, WIN_COLS], BF16)
        w_uq_bf = sb("w_uq_bf", [128, 4, 4, 256], BF16)
        w_ukvk_bf = sb("w_ukvk_bf", [128, 2, 4, 128], BF16)
        w_ukvv_bf = sb("w_ukvv_bf", [128, 2, 512], BF16)
        pool_w_bf = sb("pool_w_bf", [128, 4, 128], BF16)
        w_out_bf = sb("w_out_bf", [128, 8, 1024], BF16)
        mats_bf = sb("mats_bf", [128, 3, 128], BF16)
        lng_bc = sb("lng_bc", [128, 1024], F32)
        lnb_bc = sb("lnb_bc", [128, 1024], F32)
        cols = sb("cols", [128, 8], F32)
        gq = sb("gq", [128, 4], F32)
        gkv = sb("gkv", [128, 2], F32)
        pscale = sb("pscale", [128, 4], F32)
        poolc = sb("poolc", [128, 4, HALO], F32)
        smalls = sb("smalls", [128, 8], F32)
        Kn_flat = sb("Kn", [128, 4 * S], BF16)
        Kn = Kn_flat[:].rearrange("p (a b) -> p a b", a=4)
        kdup = sb("kdup", [128, S], BF16)
        Vaug_flat = sb("Vaug", [128, 16 * 4 * 129], BF16)
        Vaug = Vaug_flat[:].rearrange("p (a b c) -> p a b c", a=16, b=4)
        xbf = sb("xbf", [128, 4, 1024], BF16)
        xT = sb("xT", [128, 8, TG], BF16)
        xq_bf = sb("xq_bf", [128, 4, TG], BF16)
        xkv_bf = sb("xkv_bf", [128, 2, TG], BF16)
        sq_bf = sb("sq_bf", [128, 2, TG], BF16)
        rinvq = sb("rinvq", [128, TG], F32)
        rinvkv = sb("rinvkv", [128, TG], F32)
        rinvcol = sb("rinvcol", [128, 4], F32)
        tabT = sb("tabT", [128, TG], F32)
        tabu = sb("tabu", [128, TG], F32)
        qn = sb("qn", [128, 4, TG], BF16)
        qr = sb("qr", [128, 4, TG], BF16)
        sga = sb("sga", [128, 4, TG], BF16)
        sgb = sb("sgb", [128, 4, TG], BF16)
        ubuf = sb("ubuf", [128, 4, HALO + TG], F32)
        pws = sb("pws", [128, 2, HALO + TG], F32)
        pooled_on = sb("pooled_on", [128, 4, TG], BF16)
        posi = sb("posi", [128, TG], I32)
        pT = sb("pT", [128, 4, TG], BF16)
        rs = sb("rs", [128, 4, 2], F32)
        stab = sb("stab", [128, 6, 4], F32)
        yT = sb("yT", [128, 8, TG], BF16)
        xres = sb("xres", [128, 4, 1024], F32)
        stats = sb("stats", [128, 4, 12], F32)
        mv = sb("mv", [128, 4, 4], F32)

        ident = mats_bf[:, 0, :]
        negmask = mats_bf[:, 1, :]
        ones = mats_bf[:, 2, :]
        eps_rms = smalls[:, 0:1]
        eps_ln = smalls[:, 1:2]
        zero_col = smalls[:, 2:3]

        B = {}

        def bf(name):
            if name not in B:
                B[name] = Buf(name)
            return B[name]

        op(POOL, lambda: nc.gpsimd.memset(smalls[:, 0:1], RMS_EPS), writes=[bf("smalls")])
        op(POOL, lambda: nc.gpsimd.memset(smalls[:, 1:2], LN_EPS), writes=[bf("smalls")])
        op(POOL, lambda: nc.gpsimd.memset(smalls[:, 2:3], 0.0), writes=[bf("smalls")])
        op(POOL, lambda: nc.gpsimd.memset(ubuf[:, :, 0:HALO], 0.0), writes=[bf(f"u{g}") for g in range(4)])

        d_c = dsem("consts")
        dma(SP, d_c, cols[:], cst_cols, writes=[bf("consts")])
        dma(SP, d_c, poolc[:].rearrange("p g t -> p (g t)"), pool_c, writes=[bf("consts")])
        d_ln = dsem("lnc")
        d_g = dsem("constg")

        def load_slow_consts():
            dma(SP, d_g, gq[:], q_norm_g.rearrange("(kc p) -> p kc", p=128), writes=[bf("constg")],
                allow_slow_non_contiguous=True)
            dma(SP, d_g, gkv[:], kv_norm_g.rearrange("(kc p) -> p kc", p=128), writes=[bf("constg")],
                allow_slow_non_contiguous=True)
            dma(SP, d_g, pscale[:], pool_scale.rearrange("(g p) -> p g", p=128), writes=[bf("constg")],
                allow_slow_non_contiguous=True)

        d_m = dsem("mats")
        dma(POOL, d_m, mats_bf[:], cst_mats, writes=[bf("mats")])

        xb_sem = [dsem(f"xbf{i}") for i in range(4)]
        xr_sem = [dsem(f"xres{i}") for i in range(4)]
        pos_sem = dsem("pos")
        st_sem = [dsem(f"st{i}") for i in range(4)]

        def load_xbf(si, tg):
            for tt in range(4):
                t0 = tg * TG + tt * 128
                dma(POOL, xb_sem[tt], xbf[:, tt, :], x[si, t0:t0 + 128, :], writes=[bf(f"xbf{tt}")])

        w_in_v = w_in.rearrange("(kc p) c -> p kc c", p=128)
        SEG = {}
        for (nm, dst, src, n) in (("kr", C_KR, 768, 64), ("u", C_U, 1344, 512), ("ga", C_GA, 832, 512), ("gb", C_GB, 1856, 512),
                                  ("xq", C_XQ, 0, 512), ("xkv", C_XKV, 512, 256)):
            if nm == "u":
                for i in range(4):
                    dma(POOL, dsem(f"w_in_u{i}"), w_in_bf[:, :, dst + 128 * i:dst + 128 * (i + 1)],
                        w_in_v[:, :, src + 128 * i:src + 128 * (i + 1)], writes=[bf(f"w_in_u{i}")])
                    SEG[dst + 128 * i] = [bf(f"w_in_u{i}")]
                continue
            dma(POOL, dsem("w_in_" + nm), w_in_bf[:, :, dst:dst + n], w_in_v[:, :, src:src + n], writes=[bf("w_in_" + nm)])
            for c in range(dst, dst + n, 128):
                SEG[c] = [bf("w_in_" + nm)]
        SEG[C_KR] = [bf("w_in_kr"), bf("w_in_rot")]

        d_st1, d_st2, d_st3 = dsem("stage1"), dsem("stage2"), dsem("stage3")
        stg_uq = Vaug_flat[:, 0:4 * 768 * 2].bitcast(F32).rearrange("p (k c) -> p k c", k=4)
        stg_ukv = Kn_flat[:, 0:2 * 1024 * 2].bitcast(F32).rearrange("p (k c) -> p k c", k=2)
        stg_pw = kdup[:, 0:1024].bitcast(F32).rearrange("p (g c) -> p g c", g=4)
        state = {"bank": 0, "pt": 0, "sq": 0}

        def next_bank(ring):
            i = ring[state["bank"] % len(ring)]
            state["bank"] += 1
            return i

        ringA = [0, 1, 5, 6, 7]
        ringC = [0, 1, 2, 3, 4, 5, 6, 7]
        ringS = [0, 1, 2, 3]

        XT_ALL = [bf(f"xT{tt}_{half}") for tt in range(4) for half in range(2)]

        def build_table(si, tg):
            tok0 = tg * TG
            dma(SP, pos_sem, posi[:], positions[si:si + 1, tok0:tok0 + TG].broadcast_to([128, TG]), writes=[bf("posi")])
            op(DVE, lambda: nc.vector.tensor_copy(out=tabu[:], in_=posi[:]), reads=[bf("posi")], writes=[bf("tabu")])
            op(DVE, lambda: nc.vector.tensor_scalar(out=tabu[:], in0=tabu[:], scalar1=cols[:, 0:1], scalar2=cols[:, 1:2],
                                                    op0=ALU.mult, op1=ALU.add),
               reads=[bf("tabu"), bf("consts")], writes=[bf("tabu")])
            op(DVE, lambda: nc.vector.tensor_copy(out=tabT[:], in_=posi[:]), reads=[bf("posi")], writes=[bf("tabT")])
            op(DVE, lambda: nc.vector.scalar_tensor_tensor(out=tabu[:], in0=tabT[:], scalar=cols[:, 2:3], in1=tabu[:],
                                                           op0=ALU.mult, op1=ALU.add),
               reads=[bf("tabT"), bf("tabu"), bf("consts")], writes=[bf("tabu")])
            op(DVE, lambda: nc.vector.tensor_copy(out=posi[:], in_=tabu[:]), reads=[bf("tabu")], writes=[bf("posi")])
            op(DVE, lambda: nc.vector.tensor_copy(out=tabT[:], in_=posi[:]), reads=[bf("posi")], writes=[bf("tabT")])
            op(DVE, lambda: nc.vector.tensor_tensor(out=tabu[:], in0=tabu[:], in1=tabT[:], op=ALU.subtract),
               reads=[bf("tabT"), bf("tabu")], writes=[bf("tabu")])
            op(DVE, lambda: nc.vector.scalar_tensor_tensor(out=tabu[:], in0=tabu[:], scalar=0.5, in1=tabu[:],
                                                           op0=ALU.is_gt, op1=ALU.subtract),
               reads=[bf("tabu")], writes=[bf("tabu")])

        def table_sin():
            op(ACT, lambda: nc.scalar.activation(out=tabT[:], in_=tabu[:], func=AF.Sin, scale=-6.2831845),
               reads=[bf("tabu")], writes=[bf("tabT")])

        def prep_weights():
            dma(SP, d_st1, stg_uq, w_uq.rearrange("(kc p) c -> p kc c", p=128), writes=[bf("Vstage")])
            dma(SP, d_st2, stg_ukv, w_ukv.rearrange("(kc p) c -> p kc c", p=128), writes=[bf("Kstage")])
            dma(SP, d_st3, stg_pw, pool_w.rearrange("g c d -> c g d"), writes=[bf("kdstage")])
            load_slow_consts()
            dma(SP, d_ln, lng_bc[:], ln_g[0:1, :].broadcast_to([128, 1024]), writes=[bf("lnc")])
            dma(SP, d_ln, lnb_bc[:], ln_b[0:1, :].broadcast_to([128, 1024]), writes=[bf("lnc")])
            op(DVE, lambda: nc.vector.tensor_scalar(out=w_in_bf[:, :, C_ROT:C_ROT + 32], in0=w_in_bf[:, :, C_KR + 32:C_KR + 64],
                                                    scalar1=-1.0, scalar2=None, op0=ALU.mult),
               reads=[bf("w_in_kr")], writes=[bf("w_in_rot")])
            op(DVE, lambda: nc.vector.tensor_copy(out=w_in_bf[:, :, C_ROT + 32:C_ROT + 64], in_=w_in_bf[:, :, C_KR:C_KR + 32]),
               reads=[bf("w_in_kr")], writes=[bf("w_in_rot")])
            for kc in range(4):
                sv = stg_uq[:, kc, :].rearrange("p (h c) -> p h c", h=4)
                op(DVE, lambda kc=kc, sv=sv: nc.vector.tensor_scalar(out=w_uq_bf[:, kc, :, 0:192], in0=sv, scalar1=gq[:, kc:kc + 1],
                                                                      scalar2=None, op0=ALU.mult),
                   reads=[bf("Vstage"), bf("constg")], writes=[bf("w_uq")])
                op(DVE, lambda kc=kc, sv=sv: nc.vector.tensor_scalar(out=w_uq_bf[:, kc, :, 192:224], in0=sv[:, :, 160:192],
                                                                      scalar1=gq[:, kc:kc + 1], scalar2=-1.0, op0=ALU.mult, op1=ALU.mult),
                   reads=[bf("Vstage"), bf("constg")], writes=[bf("w_uq")])
                op(DVE, lambda kc=kc, sv=sv: nc.vector.tensor_scalar(out=w_uq_bf[:, kc, :, 224:256], in0=sv[:, :, 128:160],
                                                                      scalar1=gq[:, kc:kc + 1], scalar2=None, op0=ALU.mult),
                   reads=[bf("Vstage"), bf("constg")], writes=[bf("w_uq")])
            op(POOL, lambda: nc.gpsimd.memset(Vaug[:, :, :, 128:129], 1.0), writes=[bf("Vones"), bf("Vstage")])
            for kc in range(2):
                sv = stg_ukv[:, kc, :].rearrange("p (h c) -> p h c", h=4)
                op(DVE, lambda kc=kc, sv=sv: nc.vector.tensor_scalar(out=w_ukvk_bf[:, kc, :, :], in0=sv[:, :, 0:128], scalar1=gkv[:, kc:kc + 1],
                                                                      scalar2=None, op0=ALU.mult),
                   reads=[bf("Kstage"), bf("constg")], writes=[bf("w_ukv")])
                op(DVE, lambda kc=kc, sv=sv: nc.vector.tensor_scalar(out=w_ukvv_bf[:, kc, :].rearrange("p (h c) -> p h c", h=4),
                                                                      in0=sv[:, :, 128:256], scalar1=gkv[:, kc:kc + 1],
                                                                      scalar2=None, op0=ALU.mult),
                   reads=[bf("Kstage"), bf("constg")], writes=[bf("w_ukv")])
            for g in range(4):
                op(DVE, lambda g=g: nc.vector.tensor_scalar(out=pool_w_bf[:, g, :], in0=stg_pw[:, g, :], scalar1=1.0 / POOL_W[g],
                                                            scalar2=None, op0=ALU.mult),
                   reads=[bf("kdstage")], writes=[bf("pool_w")])


        def nxt(si, tg):
            return (si, tg + 1) if tg + 1 < NTG else (si + 1, 0)

        def emit_xT(si, tg, ring):
            for tt in range(4):
                for half in range(2):
                    bi = next_bank(ring)
                    pv = banks[bi][:].bitcast(BF16)[:, 0:512].rearrange("p (a b) -> p a b", a=4)
                    for j in range(4):
                        kc = half * 4 + j
                        op(PE, lambda pv=pv, j=j, kc=kc, tt=tt: nc.tensor.transpose(out=pv[:, j, :], in_=xbf[:, tt, kc * 128:(kc + 1) * 128],
                                                                                   identity=ident),
                           reads=[bf(f"xbf{tt}"), bf("mats")], writes=[bbuf[bi]], inc=(j == 3))
                    op(DVE, lambda pv=pv, half=half, tt=tt: nc.vector.tensor_copy(out=xT[:, half * 4:half * 4 + 4, tt * 128:(tt + 1) * 128], in_=pv),
                       reads=[bbuf[bi]], writes=[bf(f"xT{tt}_{half}")])
            n2 = nxt(si, tg)
            if n2[0] < nseq:
                load_xbf(*n2)

        for tt in range(4):
            dma(SP, xr_sem[tt], xres[:, tt, :], x[0, tt * 128:(tt + 1) * 128, :], writes=[bf(f"xres{tt}")])
        for tt in range(4):
            op(DVE, lambda tt=tt: nc.vector.tensor_copy(out=xbf[:, tt, :], in_=xres[:, tt, :]),
               reads=[bf(f"xres{tt}")], writes=[bf(f"xbf{tt}")])
        emit_xT(0, 0, ringA)
        build_table(0, 0)
        dma(POOL, dsem("w_out"), w_out_bf[:], w_out.rearrange("(kc p) c -> p kc c", p=128), writes=[bf("w_out")])
        prep_weights()

        def ln_c2_norm(si, tg, tt):
            xs = tt
            op(DVE, lambda: nc.vector.scalar_tensor_tensor(out=xres[:, xs, :], in0=xres[:, xs, :], scalar=mv[:, xs, 3:4], in1=lng_bc[:],
                                                           op0=ALU.add, op1=ALU.mult),
               reads=[bf(f"xres{xs}"), bf(f"mvc{xs}"), bf("lnc")], writes=[bf(f"xres{xs}")])

        def ln_c2_gb(si, tg, tt):
            t0 = tg * TG + tt * 128
            xs = tt
            op(DVE, lambda: nc.vector.scalar_tensor_tensor(out=xres[:, xs, :], in0=xres[:, xs, :], scalar=mv[:, xs, 2:3], in1=lnb_bc[:],
                                                           op0=ALU.mult, op1=ALU.add),
               reads=[bf(f"xres{xs}"), bf(f"mvb{xs}"), bf("lnc")], writes=[bf(f"xres{xs}")])
            dma(SP, st_sem[xs], out[si, t0:t0 + 128, :], xres[:, xs, :], reads=[bf(f"xres{xs}")])

        def ln_c2(si, tg, tt):
            ln_c2_norm(si, tg, tt)
            ln_c2_gb(si, tg, tt)

        tail_pending = []
        for si in range(nseq):
            for tg in range(NTG):
                tok0 = tg * TG
                if tg == 0:
                    op(DVE, lambda: nc.vector.memset(stab[:, 1, :], 0.0), writes=[bf("knmx")])
                    op(DVE, lambda: nc.vector.memset(stab[:, 4, :], 0.0), writes=[bf("krmx")])
                nsi, ntg = (si, tg + 1) if tg + 1 < NTG else (si + 1, 0)
                last_tg = (si == nseq - 1 and tg == NTG - 1)

                def inproj_chunk(c0):
                    bi = next_bank(ringA)
                    for kc in range(8):
                        op(PE, lambda kc=kc, bi=bi: nc.tensor.matmul(banks[bi][:], lhsT=w_in_bf[:, kc, c0:c0 + 128], rhs=xT[:, kc, :],
                                                                     start=(kc == 0), stop=(kc == 7)),
                           reads=XT_ALL + SEG[c0], writes=[bbuf[bi]], inc=(kc == 7))
                    return bi

                for g in range(4):
                    w = POOL_W[g]
                    bi = inproj_chunk(C_U + g * 128)
                    op(ACT, lambda bi=bi, g=g: nc.scalar.activation(out=ubuf[:, g, HALO:HALO + TG], in_=banks[bi][:], func=AF.Copy),
                       reads=[bbuf[bi]], writes=[bf(f"u{g}")])
                    L = HALO + TG
                    src = ubuf[:, g, :]
                    nsteps = g + 1
                    sh = 1
                    lo = HALO - (w - 1)
                    cur_buf = bf(f"u{g}")
                    for st in range(nsteps):
                        dst = pws[:, st % 2, :]
                        dbuf = bf(f"pws{st % 2}")
                        lo2 = lo + sh
                        op(DVE, lambda dst=dst, src=src, lo2=lo2, sh=sh, L=L: nc.vector.tensor_tensor(
                            out=dst[:, lo2:L], in0=src[:, lo2:L], in1=src[:, lo2 - sh:L - sh], op=ALU.add),
                           reads=[cur_buf], writes=[dbuf])
                        src, cur_buf, lo, sh = dst, dbuf, lo2, sh * 2
                    if tg == 0:
                        op(DVE, lambda src=src, g=g: nc.vector.tensor_tensor(out=src[:, HALO:2 * HALO], in0=src[:, HALO:2 * HALO],
                                                                              in1=poolc[:, g, :], op=ALU.mult),
                           reads=[cur_buf, bf("consts")], writes=[cur_buf])
                    op(DVE, lambda g=g: nc.vector.tensor_copy(out=ubuf[:, g, 0:HALO], in_=ubuf[:, g, TG:TG + HALO]),
                       reads=[bf(f"u{g}")], writes=[bf(f"u{g}")])
                    op(DVE, lambda src=src, g=g, w=w: nc.vector.scalar_tensor_tensor(out=pooled_on[:, g, :], in0=ubuf[:, g, HALO:HALO + TG],
                                                                                    scalar=-float(w), in1=src[:, HALO:HALO + TG],
                                                                                    op0=ALU.mult, op1=ALU.add),
                       reads=[cur_buf, bf(f"u{g}")], writes=[bf(f"po{g}")])
                if tg == NTG - 1:
                    op(DVE, lambda: nc.vector.memset(ubuf[:, :, 0:HALO], 0.0), writes=[bf(f"u{g}") for g in range(4)])

                while tail_pending:
                    tail_pending.pop(0)()
                for c in range(4):
                    bi = inproj_chunk(C_GA + c * 128)
                    op(ACT, lambda bi=bi, c=c: nc.scalar.activation(out=sga[:, c, :], in_=banks[bi][:], func=AF.Silu),
                       reads=[bbuf[bi]], writes=[bf(f"sga{c}")])
                for c in range(4):
                    bi = inproj_chunk(C_GB + c * 128)
                    op(ACT, lambda bi=bi, c=c: nc.scalar.activation(out=sgb[:, c, :], in_=banks[bi][:], func=AF.Silu),
                       reads=[bbuf[bi]], writes=[bf(f"sgb{c}")])
                table_sin()
                op(ACT, lambda: nc.scalar.activation(out=smalls[:, 5:6], in_=smalls[:, 0:1], func=AF.Ln),
                   reads=[bf("smalls")], writes=[bf("dummy")])
                ssq_bank = {"q": 2, "kv": 3}
                colbank = 4
                sqi = 0
                pending = []

                def flush_pending():
                    while pending:
                        pending.pop(0)()

                def ssq_q_mm(sl, c):
                    op(PE, lambda: nc.tensor.matmul(banks[ssq_bank["q"]][:], lhsT=ones, rhs=sq_bf[:, sl, :],
                                                    start=(c == 0), stop=(c == 3)),
                       reads=[bf(f"sq{sl}"), bf("mats")], writes=[bbuf[ssq_bank["q"]]], inc=True)

                def ssq_kv_mm(sl, c):
                    op(PE, lambda: nc.tensor.matmul(banks[ssq_bank["kv"]][:], lhsT=ones, rhs=sq_bf[:, sl, :],
                                                    start=(c == 0), stop=(c == 1)),
                       reads=[bf(f"sq{sl}"), bf("mats")], writes=[bbuf[ssq_bank["kv"]]], inc=False)
                    for tt in range(4):
                        op(PE, lambda tt=tt: nc.tensor.matmul(banks[colbank][:, tt:tt + 1], lhsT=sq_bf[:, sl, tt * 128:(tt + 1) * 128],
                                                              rhs=ones[:, 0:1], start=(c == 0 and tt == 0), stop=(c == 1),
                                                              skip_group_check=True),
                           reads=[bf(f"sq{sl}"), bf("mats")], writes=[bbuf[colbank]], inc=(tt == 3))

                for c in range(4):
                    bi = inproj_chunk(C_XQ + c * 128)
                    flush_pending()
                    op(ACT, lambda bi=bi, c=c: nc.scalar.activation(out=xq_bf[:, c, :], in_=banks[bi][:], func=AF.Copy),
                       reads=[bbuf[bi]], writes=[bf(f"xq{c}")])
                    sl = sqi % 2
                    sqi += 1
                    op(ACT, lambda bi=bi, sl=sl: nc.scalar.activation(out=sq_bf[:, sl, :], in_=banks[bi][:], func=AF.Square),
                       reads=[bbuf[bi]], writes=[bf(f"sq{sl}")])
                    pending.append(lambda sl=sl, c=c: ssq_q_mm(sl, c))
                for c in range(2):
                    bi = inproj_chunk(C_XKV + c * 128)
                    flush_pending()
                    op(ACT, lambda bi=bi, c=c: nc.scalar.activation(out=xkv_bf[:, c, :], in_=banks[bi][:], func=AF.Copy),
                       reads=[bbuf[bi]], writes=[bf(f"xkv{c}")])
                    sl = sqi % 2
                    sqi += 1
                    op(ACT, lambda bi=bi, sl=sl: nc.scalar.activation(out=sq_bf[:, sl, :], in_=banks[bi][:], func=AF.Square),
                       reads=[bbuf[bi]], writes=[bf(f"sq{sl}")])
                    pending.append(lambda sl=sl, c=c: ssq_kv_mm(sl, c))
                for g in range(4):
                    bi = next_bank(ringA)
                    op(PE, lambda g=g, bi=bi: nc.tensor.matmul(banks[bi][:], lhsT=pool_w_bf[:, g, :], rhs=pooled_on[:, g, :], start=True, stop=True),
                       reads=[bf(f"po{g}"), bf("pool_w")], writes=[bbuf[bi]])
                    op(DVE, lambda g=g, bi=bi: nc.vector.scalar_tensor_tensor(out=yT[:, 4 + g, :], in0=banks[bi][:], scalar=pscale[:, g:g + 1],
                                                                            in1=sgb[:, g, :], op0=ALU.mult, op1=ALU.mult),
                       reads=[bbuf[bi], bf(f"sgb{g}"), bf("constg")], writes=[bf(f"yT{4 + g}")])

                bi_kr = inproj_chunk(C_KR)
                flush_pending()
                op(DVE, lambda bi=bi_kr: nc.vector.tensor_tensor(out=pws[:, 0, 0:TG], in0=banks[bi][:], in1=tabT[:], op=ALU.mult),
                   reads=[bbuf[bi_kr], bf("tabT")], writes=[bf("pws0")])
                op(DVE, lambda: nc.vector.tensor_copy(out=tabu[0:64, :], in_=pws[64:128, 0, 0:TG]), reads=[bf("pws0")], writes=[bf("tabu")])
                op(DVE, lambda: nc.vector.tensor_tensor(out=kdup[0:64, tok0:tok0 + TG], in0=pws[0:64, 0, 0:TG], in1=tabu[0:64, :], op=ALU.add),
                   reads=[bf("pws0"), bf("tabu")], writes=[bf("kdup"), bf("kdstage")])
                op(DVE, lambda: nc.vector.tensor_copy(out=kdup[64:128, tok0:tok0 + TG], in_=kdup[0:64, tok0:tok0 + TG]),
                   reads=[bf("kdup")], writes=[bf("kdup")])

                op(ACT, lambda: nc.scalar.activation(out=rinvq[:], in_=banks[ssq_bank["q"]][:], func=AF.Ln, bias=eps_rms, scale=1.0 / 512),
                   reads=[bbuf[ssq_bank["q"]], bf("smalls")], writes=[bf("rinvq")])
                op(ACT, lambda: nc.scalar.activation(out=rinvq[:], in_=rinvq[:], func=AF.Exp, scale=-0.5),
                   reads=[bf("rinvq")], writes=[bf("rinvq")])
                op(ACT, lambda: nc.scalar.activation(out=rinvkv[:], in_=banks[ssq_bank["kv"]][:], func=AF.Ln, bias=eps_rms, scale=1.0 / 256),
                   reads=[bbuf[ssq_bank["kv"]], bf("smalls")], writes=[bf("rinvkv")])
                op(ACT, lambda: nc.scalar.activation(out=rinvkv[:], in_=rinvkv[:], func=AF.Exp, scale=-0.5),
                   reads=[bf("rinvkv")], writes=[bf("rinvkv")])
                op(ACT, lambda: nc.scalar.activation(out=rinvcol[:], in_=banks[colbank][:, 0:4], func=AF.Ln, bias=eps_rms, scale=1.0 / 256),
                   reads=[bbuf[colbank], bf("smalls")], writes=[bf("rinvcol")])
                op(ACT, lambda: nc.scalar.activation(out=rinvcol[:], in_=rinvcol[:], func=AF.Exp, scale=-0.5),
                   reads=[bf("rinvcol")], writes=[bf("rinvcol")])
                op(DVE, lambda: nc.vector.tensor_tensor(out=tabT[:], in0=tabT[:], in1=rinvq[:], op=ALU.mult),
                   reads=[bf("tabT"), bf("rinvq")], writes=[bf("tabT")])

                st_q, st_k = [], []
                qslots = [(pT[:, i, :], bf(f"pT{i}")) for i in range(4)] + [(pooled_on[:, i, :], bf(f"po{i}")) for i in range(4)]
                kslots = [(sq_bf[:, 0, :], bf("sq0")), (sq_bf[:, 1, :], bf("sq1")), (xq_bf[:, 0, :], bf("xq0")),
                          (xq_bf[:, 1, :], bf("xq1")), (xq_bf[:, 2, :], bf("xq2"))]

                def st_norm(parts, row, col, running, slots, pend):
                    mine = []
                    for (src_ap, src_bufs, sc) in parts:
                        dst_ap, dst_buf = slots.pop(0)
                        mine.append((dst_ap, dst_buf))
                        op(ACT, lambda src_ap=src_ap, dst_ap=dst_ap, sc=sc: nc.scalar.activation(out=dst_ap, in_=src_ap, func=AF.Square, scale=sc),
                           reads=src_bufs, writes=[dst_buf])

                    def later():
                        bi = next_bank(ringA)
                        for i, (dst_ap, dst_buf) in enumerate(mine):
                            op(PE, lambda dst_ap=dst_ap, i=i: nc.tensor.matmul(banks[bi][:], lhsT=ones, rhs=dst_ap,
                                                                               start=(i == 0), stop=(i == len(mine) - 1)),
                               reads=[dst_buf, bf("mats")], writes=[bbuf[bi]], inc=True)
                        if running:
                            op(DVE, lambda: nc.vector.reduce_max(out=stab[:, 5, col:col + 1], in_=banks[bi][:], axis=mybir.AxisListType.X),
                               reads=[bbuf[bi]], writes=[bf("sttmp")])
                            op(DVE, lambda: nc.vector.tensor_tensor(out=stab[:, row, col:col + 1], in0=stab[:, row, col:col + 1],
                                                                    in1=stab[:, 5, col:col + 1], op=ALU.max),
                               reads=[bf("sttmp")], writes=[bf("knmx" if row == 1 else "krmx")])
                        else:
                            op(DVE, lambda: nc.vector.reduce_max(out=stab[:, row, col:col + 1], in_=banks[bi][:], axis=mybir.AxisListType.X),
                               reads=[bbuf[bi]], writes=[bf("qmx")])
                    pend.append(later)

                def st_chain(c0, c1):
                    tg_ = f"{c0}"
                    op(DVE, lambda: nc.vector.tensor_scalar(out=stab[:, 2, c0:c1], in0=stab[:, 1, c0:c1], scalar1=stab[:, 4, 0:1], scalar2=None, op0=ALU.add),
                       reads=[bf("knmx"), bf("krmx")], writes=[bf("sttmp2" + tg_)])
                    op(DVE, lambda: nc.vector.tensor_tensor(out=stab[:, 2, c0:c1], in0=stab[:, 2, c0:c1], in1=stab[:, 0, c0:c1], op=ALU.mult),
                       reads=[bf("sttmp2" + tg_), bf("qmx")], writes=[bf("sttmp2" + tg_)])
                    op(ACT, lambda: nc.scalar.activation(out=stab[:, 2, c0:c1], in_=stab[:, 2, c0:c1], func=AF.Ln, bias=eps_rms, scale=1.0),
                       reads=[bf("sttmp2" + tg_), bf("smalls")], writes=[bf("sttmp2" + tg_)])
                    op(ACT, lambda: nc.scalar.activation(out=stab[:, 2, c0:c1], in_=stab[:, 2, c0:c1], func=AF.Exp, scale=0.5),
                       reads=[bf("sttmp2" + tg_)], writes=[bf("sttmp2" + tg_)])
                    op(DVE, lambda: nc.vector.tensor_scalar(out=stab[:, 3, c0:c1], in0=stab[:, 2, c0:c1], scalar1=-SCALE, scalar2=None, op0=ALU.mult),
                       reads=[bf("sttmp2" + tg_)], writes=[bf("negc" + tg_)])

                st_norm([(kdup[:, tok0:tok0 + TG], [bf("kdup")], 0.5 ** 0.5)], 4, 0, True, kslots, st_k)
                XQ = [bf(f"xq{c}") for c in range(4)]
                XKV = [bf(f"xkv{c}") for c in range(2)]
                for h in range(4):
                    for part in range(2):
                        bi = next_bank(ringA)
                        c0 = 0 if part == 0 else 128
                        for kc in range(4):
                            op(PE, lambda kc=kc, bi=bi, h=h, c0=c0: nc.tensor.matmul(banks[bi][:], lhsT=w_uq_bf[:, kc, h, c0:c0 + 128], rhs=xq_bf[:, kc, :],
                                                                                    start=(kc == 0), stop=(kc == 3)),
                               reads=XQ + [bf("w_uq")], writes=[bbuf[bi]], inc=(kc == 3))
                        if part == 0:
                            op(DVE, lambda bi=bi, h=h: nc.vector.tensor_tensor(out=qn[:, h, :], in0=banks[bi][:], in1=rinvq[:], op=ALU.mult),
                               reads=[bbuf[bi], bf("rinvq")], writes=[bf(f"qn{h}")])
                        else:
                            op(DVE, lambda bi=bi, h=h: nc.vector.tensor_tensor(out=qr[:, h, :], in0=banks[bi][:], in1=tabT[:], op=ALU.mult),
                               reads=[bbuf[bi], bf("tabT")], writes=[bf(f"qr{h}")])
                    st_norm([(qn[:, h, :], [bf(f"qn{h}")], 1.0), (qr[:, h, :], [bf(f"qr{h}")], 1.0)], 0, h, False, qslots, st_q)
                for h in range(4):
                    bi = next_bank(ringA)
                    for kc in range(2):
                        op(PE, lambda kc=kc, bi=bi, h=h: nc.tensor.matmul(banks[bi][:], lhsT=w_ukvk_bf[:, kc, h, :], rhs=xkv_bf[:, kc, :],
                                                                         start=(kc == 0), stop=(kc == 1)),
                           reads=XKV + [bf("w_ukv")], writes=[bbuf[bi]], inc=(kc == 1))
                    op(DVE, lambda bi=bi, h=h: nc.vector.tensor_tensor(out=Kn[:, h, tok0:tok0 + TG], in0=banks[bi][:], in1=rinvkv[:], op=ALU.mult),
                       reads=[bbuf[bi], bf("rinvkv")], writes=[bf(f"Kn{h}"), bf("Kstage")])
                    st_norm([(Kn[:, h, tok0:tok0 + TG], [bf(f"Kn{h}")], 1.0)], 1, h, True, kslots, st_k)
                    if st_q:
                        st_q.pop(0)()
                while st_q:
                    st_q.pop(0)()
                for tt in range(4):
                    bi = next_bank(ringA)
                    kb = tg * 4 + tt
                    for kc in range(2):
                        op(PE, lambda kc=kc, bi=bi, tt=tt: nc.tensor.matmul(banks[bi][:], lhsT=xkv_bf[:, kc, tt * 128:(tt + 1) * 128], rhs=w_ukvv_bf[:, kc, :],
                                                                           start=(kc == 0), stop=(kc == 1)),
                           reads=XKV + [bf("w_ukv")], writes=[bbuf[bi]], inc=(kc == 1))
                    if st_k:
                        st_k.pop(0)()
                    if tt == 3:
                        st_chain(0, 3)
                    if tt < 2:
                        op(ACT, lambda bi=bi, tt=tt, kb=kb: nc.scalar.activation(out=Vaug[:, kb, :, 0:128],
                                                                                in_=banks[bi][:].rearrange("p (h c) -> p h c", h=4),
                                                                                func=AF.Identity, scale=rinvcol[:, tt:tt + 1]),
                           reads=[bbuf[bi], bf("rinvcol")], writes=[bf("V"), bf("Vstage")])
                    else:
                        op(DVE, lambda bi=bi, tt=tt, kb=kb: nc.vector.tensor_scalar(out=Vaug[:, kb, :, 0:128],
                                                                                   in0=banks[bi][:].rearrange("p (h c) -> p h c", h=4),
                                                                                   scalar1=rinvcol[:, tt:tt + 1], scalar2=None, op0=ALU.mult),
                           reads=[bbuf[bi], bf("rinvcol")], writes=[bf("V"), bf("Vstage")])

                while st_k:
                    st_k.pop(0)()
                st_chain(3, 4)

                for tt in range(4):
                    t0 = tok0 + tt * 128
                    dma(SP, xr_sem[tt], xres[:, tt, :], x[si, t0:t0 + 128, :], writes=[bf(f"xres{tt}")])
                if nsi < nseq:
                    build_table(nsi, ntg)
                nkb = 4 * (tg + 1)
                steps = []

                def emit_pv(h, kb, slot, part=None):
                    pvset = (4, 5) if h % 2 == 0 else (6, 7)
                    j = kb - 4 * tg
                    qls = list(range(max(j, 0), 4))
                    if part == 0:
                        qls = qls[:len(qls) // 2]
                    elif part == 1:
                        qls = qls[len(qls) // 2:]
                    for ql in qls:
                        pb = pvset[ql // 2]
                        gcol = (ql % 2) * 129
                        op(PE, lambda ql=ql, pb=pb, gcol=gcol: nc.tensor.matmul(
                            banks[pb][:, gcol:gcol + 129], lhsT=pT[:, slot, ql * 128:(ql + 1) * 128], rhs=Vaug[:, kb, h, :],
                            start=(kb == 0 and ql % 2 == 0), stop=(kb == 4 * tg + ql), skip_group_check=True),
                           reads=[bf(f"pT{slot}"), bf("V"), bf("Vones")], writes=[bbuf[pb]], inc=(ql == 3))
                    if j in (1, 3) and part != 0:
                        half = j // 2
                        pb = pvset[half]
                        pv3 = banks[pb][:, 0:258].rearrange("p (g c) -> p g c", g=2)
                        op(DVE, lambda: nc.vector.reciprocal(out=rs[:, h, :], in_=pv3[:, :, 128]),
                           reads=[bbuf[pb]], writes=[bf(f"rs{h}")])
                        op(DVE, lambda: nc.vector.tensor_tensor(
                            out=pooled_on[:, 2 * half:2 * half + 2, h * 128:(h + 1) * 128], in0=pv3[:, :, 0:128],
                            in1=rs[:, h, :].unsqueeze(2).to_broadcast([128, 2, 128]), op=ALU.mult),
                           reads=[bbuf[pb], bf(f"rs{h}")], writes=[bf(f"po{2 * half}"), bf(f"po{2 * half + 1}")])

                for h in range(4):
                    for kb in range(nkb):
                        j = kb - 4 * tg
                        c0 = 128 * j if j > 0 else 0
                        bi = next_bank(ringS)
                        slot = state["pt"] % 4
                        state["pt"] += 1
                        op(PE, lambda bi=bi, kb=kb, c0=c0, h=h: nc.tensor.matmul(banks[bi][:, c0:TG], lhsT=Kn[:, h, kb * 128:(kb + 1) * 128], rhs=qn[:, h, c0:TG],
                                                                                start=True, stop=False),
                           reads=[bf(f"Kn{h}"), bf(f"qn{h}")], writes=[bbuf[bi]], inc=False)
                        st = steps.pop(0) if len(steps) >= 3 else None
                        if st is not None:
                            emit_pv(*st, part=0)
                        op(PE, lambda bi=bi, kb=kb, c0=c0, h=h, j=j: nc.tensor.matmul(banks[bi][:, c0:TG], lhsT=kdup[:, kb * 128:(kb + 1) * 128], rhs=qr[:, h, c0:TG],
                                                                                     start=False, stop=(j < 0)),
                           reads=[bf("kdup"), bf(f"qr{h}")], writes=[bbuf[bi]], inc=(j < 0))
                        if j >= 0:
                            op(PE, lambda bi=bi, c0=c0: nc.tensor.matmul(banks[bi][:, c0:c0 + 128], lhsT=ident, rhs=negmask, start=False, stop=True),
                               reads=[bf("mats")], writes=[bbuf[bi]], inc=True)
                        op(ACT, lambda bi=bi, slot=slot, c0=c0, h=h: nc.scalar.activation(out=pT[:, slot, c0:TG], in_=banks[bi][:, c0:TG], func=AF.Exp,
                                                                                    bias=stab[:, 3, h:h + 1], scale=SCALE),
                           reads=[bbuf[bi], bf("negc0" if h < 3 else "negc3")], writes=[bf(f"pT{slot}")])
                        if st is not None:
                            emit_pv(*st, part=1)
                        steps.append((h, kb, slot))
                while steps:
                    emit_pv(*steps.pop(0))
                if nsi < nseq:
                    emit_xT(nsi, ntg, ringS)

                for tt in range(4):
                    bi = next_bank(ringC)
                    pv = banks[bi][:].bitcast(BF16)[:, 0:512].rearrange("p (a b) -> p a b", a=4)
                    for h in range(4):
                        op(PE, lambda pv=pv, h=h, tt=tt: nc.tensor.transpose(out=pv[:, h, :], in_=pooled_on[:, tt, h * 128:(h + 1) * 128], identity=ident),
                           reads=[bf(f"po{tt}"), bf("mats")], writes=[bbuf[bi]], inc=(h == 3))
                    op(DVE, lambda pv=pv, tt=tt: nc.vector.tensor_tensor(out=yT[:, 0:4, tt * 128:(tt + 1) * 128], in0=pv,
                                                                        in1=sga[:, :, tt * 128:(tt + 1) * 128], op=ALU.mult),
                       reads=[bbuf[bi]] + [bf(f"sga{c}") for c in range(4)], writes=[bf(f"yTa{tt}")])
                for tt in range(4):
                    t0 = tok0 + tt * 128
                    xs = tt
                    if last_tg and tt > 0:
                        ln_c2_norm(si, tg, tt - 1)
                    YDEPS = [bf(f"yTa{tt}")] + [bf(f"yT{4 + g}") for g in range(4)] + [bf("w_out")]
                    for half in range(2):
                        bi = next_bank(ringC)
                        for c in range(8):
                            op(PE, lambda c=c, bi=bi, half=half, tt=tt: nc.tensor.matmul(banks[bi][:], lhsT=yT[:, c, tt * 128:(tt + 1) * 128],
                                                                                        rhs=w_out_bf[:, c, half * 512:(half + 1) * 512],
                                                                                        start=(c == 0), stop=(c == 7)),
                               reads=YDEPS, writes=[bbuf[bi]], inc=(c == 7))
                        op(DVE, lambda bi=bi, half=half, xs=xs: nc.vector.scalar_tensor_tensor(
                            out=xres[:, xs, half * 512:(half + 1) * 512], in0=xres[:, xs, half * 512:(half + 1) * 512], scalar=ALPHA,
                            in1=banks[bi][:], op0=ALU.mult, op1=ALU.add),
                           reads=[bbuf[bi], bf(f"xres{xs}")], writes=[bf(f"xres{xs}")])
                        op(DVE, lambda half=half, xs=xs: nc.vector.bn_stats(out=stats[:, xs, half * 6:(half + 1) * 6],
                                                                           in_=xres[:, xs, half * 512:(half + 1) * 512]),
                           reads=[bf(f"xres{xs}")], writes=[bf(f"stats{xs}")])
                    op(DVE, lambda xs=xs: nc.vector.bn_aggr(out=mv[:, xs, 0:2], in_=stats[:, xs, :]),
                       reads=[bf(f"stats{xs}")], writes=[bf(f"mv{xs}")])
                    op(DVE, lambda xs=xs: nc.vector.tensor_scalar(out=mv[:, xs, 3:4], in0=mv[:, xs, 0:1], scalar1=-1.0, scalar2=None, op0=ALU.mult),
                       reads=[bf(f"mv{xs}")], writes=[bf(f"mvc{xs}")])
                    op(ACT, lambda xs=xs: nc.scalar.activation(out=mv[:, xs, 2:3], in_=mv[:, xs, 1:2], func=AF.Ln, bias=eps_ln, scale=1.0),
                       reads=[bf(f"mv{xs}"), bf("smalls")], writes=[bf(f"mvb{xs}")])
                    op(ACT, lambda xs=xs: nc.scalar.activation(out=mv[:, xs, 2:3], in_=mv[:, xs, 2:3], func=AF.Exp, scale=-0.5),
                       reads=[bf(f"mvb{xs}")], writes=[bf(f"mvb{xs}")])
                    if last_tg and tt > 0:
                        ln_c2_gb(si, tg, tt - 1)
                if last_tg:
                    ln_c2(si, tg, 3)
                else:
                    op(ACT, lambda: nc.scalar.activation(out=smalls[:, 4:5], in_=smalls[:, 2:3], func=AF.Silu),
                       reads=[bf("smalls")], writes=[bf("dummy")])
                    for tt in range(4):
                        ln_c2(si, tg, tt)

        while tail_pending:
            tail_pending.pop(0)()
        for xs in range(4):
            nc.sync.wait_ge(st_sem[xs].sem, st_sem[xs].n)
    return nc


def _consts():
    half = 32
    inv_freq = (10000.0 ** (-np.arange(half, dtype=np.float64) / half)) / (2.0 * np.pi)
    cols = np.zeros((128, 8), np.float32)
    for p in range(128):
        hi = np.float32(inv_freq[p % 32])
        cols[p, 0] = hi
        cols[p, 2] = np.float32(inv_freq[p % 32] - np.float64(hi))
        cols[p, 1] = 0.25 if p < 64 else 0.0
    pc = np.ones((128, 4, HALO), np.float32)
    for g, w in enumerate(POOL_W):
        for t in range(HALO):
            pc[:, g, t] = w / min(t + 1, w)
    mats = np.zeros((128, 3, 128), np.float32)
    mats[:, 0, :] = np.eye(128, dtype=np.float32)
    k = np.arange(128)[:, None]
    q = np.arange(128)[None, :]
    mats[:, 1, :] = np.where(k > q, NEG, 0.0)
    mats[:, 2, :] = 1.0
    return cols, pc.reshape(128, 4 * HALO), mats


_NC_CACHE = {}


def run(inputs, nseq, ncores):
    if nseq not in _NC_CACHE:
        _NC_CACHE[nseq] = build_nc(nseq)
    nc = _NC_CACHE[nseq]
    cols, pc, mats = _consts()
    x = np.ascontiguousarray(np.asarray(inputs["x"], dtype=np.float32))
    pos = np.ascontiguousarray(np.asarray(inputs["positions"], dtype=np.int32))
    shared = {k: np.ascontiguousarray(np.asarray(inputs[k], dtype=np.float32)) for k in
              ("w_in", "q_norm_g", "w_uq", "kv_norm_g", "w_ukv", "pool_w", "pool_scale", "w_out", "ln_g", "ln_b")}
    shared.update({"cst_cols": cols, "pool_c": pc, "cst_mats": mats})
    in_maps = []
    for i in range(ncores):
        m = dict(shared)
        m["x"] = x[i * nseq:(i + 1) * nseq]
        m["positions"] = pos[i * nseq:(i + 1) * nseq]
        in_maps.append(m)
    res = run_bass_kernel_spmd(nc, in_maps, core_ids=list(range(ncores)))
    return np.concatenate([r["out"] for r in res.results], axis=0)


def kernel(x, positions, w_in, q_norm_g, w_uq, kv_norm_g, w_ukv, pool_w, pool_scale, w_out, ln_g, ln_b):
    inputs = dict(x=x, positions=positions, w_in=w_in, q_norm_g=q_norm_g, w_uq=w_uq, kv_norm_g=kv_norm_g, w_ukv=w_ukv,
                  pool_w=pool_w, pool_scale=pool_scale, w_out=w_out, ln_g=ln_g, ln_b=ln_b)
    return run(inputs, NSEQ, NCORES).astype(np.float32)
```

```python
import numpy as np
from contextlib import ExitStack
import concourse.bass as bass
import concourse.mybir as mybir
from concourse.bass_utils import run_bass_kernel_spmd

F32 = mybir.dt.float32
BF16 = mybir.dt.bfloat16
I32 = mybir.dt.int32
ALU = mybir.AluOpType
AF = mybir.ActivationFunctionType

NCORES = 8
BATCH = 32
S = 2048
D = 1024
NSEQ = BATCH // NCORES
TG = 512
NTG = S // TG
SCALE = 192.0 ** -0.5
ALPHA = 2.0 ** 0.25
RMS_EPS = 1e-6
LN_EPS = 1e-5
NEG = -30000.0
POOL_W = (2, 4, 8, 16)
HALO = 16
C_GA, C_GB, C_U, C_XQ, C_XKV, C_KR, C_ROT = 0, 512, 1024, 1536, 2048, 2304, 2368
WIN_COLS = 2432


class Tok:
    __slots__ = ("sem", "key", "val", "eng")

    def __init__(self, sem, key, val, eng):
        self.sem, self.key, self.val, self.eng = sem, key, val, eng


class Buf:
    __slots__ = ("name", "w", "r")

    def __init__(self, name):
        self.name, self.w, self.r = name, None, {}


class Eng:
    def __init__(self, name, h, sem, is_pe=False):
        self.name, self.h, self.sem, self.is_pe = name, h, sem, is_pe
        self.n = 0
        self.seen = {}


class DmaSem:
    def __init__(self, name, sem):
        self.name, self.sem, self.n = name, sem, 0


def _wait_deps(E, reads, writes):
    deps = []
    for b in reads:
        if b.w is not None:
            deps.append(b.w)
    for b in writes:
        if b.w is not None:
            deps.append(b.w)
        deps.extend(b.r.values())
    for t in deps:
        if t.eng is E and E.is_pe:
            continue
        if E.seen.get(t.key, 0) >= t.val:
            continue
        E.h.wait_ge(t.sem, t.val)
        E.seen[t.key] = t.val


def _record(tok, reads, writes):
    for b in reads:
        o = b.r.get(tok.key)
        if o is None or o.val < tok.val:
            b.r[tok.key] = tok
    for b in writes:
        b.w = tok
        b.r = {}


def op(E, fn, reads=(), writes=(), inc=True):
    _wait_deps(E, reads, writes)
    ins = fn()
    if inc:
        E.n += 1
        ins.then_inc(E.sem, 1)
        tok = Tok(E.sem, E.name, E.n, E)
    else:
        tok = Tok(E.sem, E.name, E.n + 1, E)
    _record(tok, reads, writes)
    return ins


def dma(Q, ds, out_ap, in_ap, reads=(), writes=(), **kw):
    _wait_deps(Q, reads, writes)
    ins = Q.h.dma_start(out=out_ap, in_=in_ap, **kw)
    ds.n += 16
    ins.then_inc(ds.sem, 16)
    tok = Tok(ds.sem, ds.name, ds.n, None)
    _record(tok, reads, writes)
    return ins


def build_nc(nseq=NSEQ):
    nc = bass.Bass("TRN2", target_bir_lowering=False)

    def din(name, shape, dt=F32):
        return nc.dram_tensor(name, list(shape), dt, kind="ExternalInput").ap()

    x = din("x", [nseq, S, D])
    positions = din("positions", [nseq, S], I32)
    w_in = din("w_in", [1024, 2368])
    q_norm_g = din("q_norm_g", [512])
    w_uq = din("w_uq", [512, 768])
    kv_norm_g = din("kv_norm_g", [256])
    w_ukv = din("w_ukv", [256, 1024])
    pool_w = din("pool_w", [4, 128, 128])
    pool_scale = din("pool_scale", [512])
    w_out = din("w_out", [1024, 1024])
    ln_g = din("ln_g", [1, 1024])
    ln_b = din("ln_b", [1, 1024])
    cst_cols = din("cst_cols", [128, 8])
    pool_c = din("pool_c", [128, 4 * HALO])
    cst_mats = din("cst_mats", [128, 3, 128])
    out = nc.dram_tensor("out", [nseq, S, D], F32, kind="ExternalOutput").ap()

    with ExitStack() as es:
        def sb(name, shape, dt):
            return es.enter_context(nc.sbuf_tensor(name, list(shape), dt))

        def newsem(name):
            return es.enter_context(nc.semaphore(name))

        PE = Eng("pe", nc.tensor, newsem("s_pe"), is_pe=True)
        ACT = Eng("act", nc.scalar, newsem("s_act"))
        DVE = Eng("dve", nc.vector, newsem("s_dve"))
        POOL = Eng("pool", nc.gpsimd, newsem("s_pool"))
        SP = Eng("sp", nc.sync, newsem("s_sp"))

        def dsem(name):
            return DmaSem(name, newsem("d_" + name))

        banks = [es.enter_context(nc.psum_tensor(f"pb{i}", [128, 512], F32)) for i in range(8)]
        bbuf = [Buf(f"bank{i}") for i in range(8)]

        w_in_bf = sb("w_in_bf", [128, 8, WIN_COLS], BF16)
        w_uq_bf = sb("w_uq_bf", [128, 4, 4, 256], BF16)
        w_ukvk_bf = sb("w_ukvk_bf", [128, 2, 4, 128], BF16)
        w_ukvv_bf = sb("w_ukvv_bf", [128, 2, 512], BF16)
        pool_w_bf = sb("pool_w_bf", [128, 4, 128], BF16)
        w_out_bf = sb("w_out_bf", [128, 8, 1024], BF16)
        mats_bf = sb("mats_bf", [128, 3, 128], BF16)
        lng_bc = sb("lng_bc", [128, 1024], F32)
        lnb_bc = sb("lnb_bc", [128, 1024], F32)
        cols = sb("cols", [128, 8], F32)
        gq = sb("gq", [128, 4], F32)
        gkv = sb("gkv", [128, 2], F32)
        pscale = sb("pscale", [128, 4], F32)
        poolc = sb("poolc", [128, 4, HALO], F32)
        smalls = sb("smalls", [128, 8], F32)
        Kn_flat = sb("Kn", [128, 4 * S], BF16)
        Kn = Kn_flat[:].rearrange("p (a b) -> p a b", a=4)
        kdup = sb("kdup", [128, S], BF16)
        Vaug_flat = sb("Vaug", [128, 16 * 4 * 129], BF16)
        Vaug = Vaug_flat[:].rearrange("p (a b c) -> p a b c", a=16, b=4)
        xbf = sb("xbf", [128, 4, 1024], BF16)
        xT = sb("xT", [128, 8, TG], BF16)
        xq_bf = sb("xq_bf", [128, 4, TG], BF16)
        xkv_bf = sb("xkv_bf", [128, 2, TG], BF16)
        sq_bf = sb("sq_bf", [128, 2, TG], BF16)
        rinvq = sb("rinvq", [128, TG], F32)
        rinvkv = sb("rinvkv", [128, TG], F32)
        rinvcol = sb("rinvcol", [128, 4], F32)
        tabT = sb("tabT", [128, TG], F32)
        tabu = sb("tabu", [128, TG], F32)
        qn = sb("qn", [128, 4, TG], BF16)
        qr = sb("qr", [128, 4, TG], BF16)
        sga = sb("sga", [128, 4, TG], BF16)
        sgb = sb("sgb", [128, 4, TG], BF16)
        ubuf = sb("ubuf", [128, 4, HALO + TG], F32)
        pws = sb("pws", [128, 2, HALO + TG], F32)
        pooled_on = sb("pooled_on", [128, 4, TG], BF16)
        posi = sb("posi", [128, TG], I32)
        pT = sb("pT", [128, 4, TG], BF16)
        rs = sb("rs", [128, 4, 2], F32)
        yT = sb("yT", [128, 8, TG], BF16)
        xres = sb("xres", [128, 4, 1024], F32)
        stats = sb("stats", [128, 4, 12], F32)
        mv = sb("mv", [128, 4, 4], F32)

        ident = mats_bf[:, 0, :]
        negmask = mats_bf[:, 1, :]
        ones = mats_bf[:, 2, :]
        eps_rms = smalls[:, 0:1]
        eps_ln = smalls[:, 1:2]
        zero_col = smalls[:, 2:3]

        B = {}

        def bf(name):
            if name not in B:
                B[name] = Buf(name)
            return B[name]

        op(POOL, lambda: nc.gpsimd.memset(smalls[:, 0:1], RMS_EPS), writes=[bf("smalls")])
        op(POOL, lambda: nc.gpsimd.memset(smalls[:, 1:2], LN_EPS), writes=[bf("smalls")])
        op(POOL, lambda: nc.gpsimd.memset(smalls[:, 2:3], 0.0), writes=[bf("smalls")])
        op(POOL, lambda: nc.gpsimd.memset(ubuf[:, :, 0:HALO], 0.0), writes=[bf(f"u{g}") for g in range(4)])

        d_c = dsem("consts")
        dma(SP, d_c, cols[:], cst_cols, writes=[bf("consts")])
        dma(SP, d_c, poolc[:].rearrange("p g t -> p (g t)"), pool_c, writes=[bf("consts")])
        d_ln = dsem("lnc")
        d_g = dsem("constg")

        def load_slow_consts():
            dma(SP, d_g, gq[:], q_norm_g.rearrange("(kc p) -> p kc", p=128), writes=[bf("constg")],
                allow_slow_non_contiguous=True)
            dma(SP, d_g, gkv[:], kv_norm_g.rearrange("(kc p) -> p kc", p=128), writes=[bf("constg")],
                allow_slow_non_contiguous=True)
            dma(SP, d_g, pscale[:], pool_scale.rearrange("(g p) -> p g", p=128), writes=[bf("constg")],
                allow_slow_non_contiguous=True)

        d_m = dsem("mats")
        dma(POOL, d_m, mats_bf[:], cst_mats, writes=[bf("mats")])

        xb_sem = [dsem(f"xbf{i}") for i in range(4)]
        xr_sem = [dsem(f"xres{i}") for i in range(4)]
        pos_sem = dsem("pos")
        st_sem = [dsem(f"st{i}") for i in range(4)]

        def load_xbf(si, tg):
            for tt in range(4):
                t0 = tg * TG + tt * 128
                dma(POOL, xb_sem[tt], xbf[:, tt, :], x[si, t0:t0 + 128, :], writes=[bf(f"xbf{tt}")])

        w_in_v = w_in.rearrange("(kc p) c -> p kc c", p=128)
        SEG = {}
        for (nm, dst, src, n) in (("kr", C_KR, 768, 64), ("u", C_U, 1344, 512), ("ga", C_GA, 832, 512), ("gb", C_GB, 1856, 512),
                                  ("xq", C_XQ, 0, 512), ("xkv", C_XKV, 512, 256)):
            if nm == "u":
                for i in range(4):
                    dma(POOL, dsem(f"w_in_u{i}"), w_in_bf[:, :, dst + 128 * i:dst + 128 * (i + 1)],
                        w_in_v[:, :, src + 128 * i:src + 128 * (i + 1)], writes=[bf(f"w_in_u{i}")])
                    SEG[dst + 128 * i] = [bf(f"w_in_u{i}")]
                continue
            dma(POOL, dsem("w_in_" + nm), w_in_bf[:, :, dst:dst + n], w_in_v[:, :, src:src + n], writes=[bf("w_in_" + nm)])
            for c in range(dst, dst + n, 128):
                SEG[c] = [bf("w_in_" + nm)]
        SEG[C_KR] = [bf("w_in_kr"), bf("w_in_rot")]

        d_st1, d_st2, d_st3 = dsem("stage1"), dsem("stage2"), dsem("stage3")
        stg_uq = Vaug_flat[:, 0:4 * 768 * 2].bitcast(F32).rearrange("p (k c) -> p k c", k=4)
        stg_ukv = Kn_flat[:, 0:2 * 1024 * 2].bitcast(F32).rearrange("p (k c) -> p k c", k=2)
        stg_pw = kdup[:, 0:1024].bitcast(F32).rearrange("p (g c) -> p g c", g=4)
        state = {"bank": 0, "pt": 0}

        def next_bank(ring):
            i = ring[state["bank"] % len(ring)]
            state["bank"] += 1
            return i

        ringA = [0, 1, 5, 6, 7]
        ringC = [0, 1, 2, 3, 4, 5, 6, 7]
        ringS = [0, 1, 2, 3]

        XT_ALL = [bf(f"xT{tt}_{half}") for tt in range(4) for half in range(2)]

        def build_table(si, tg):
            tok0 = tg * TG
            dma(SP, pos_sem, posi[:], positions[si:si + 1, tok0:tok0 + TG].broadcast_to([128, TG]), writes=[bf("posi")])
            op(DVE, lambda: nc.vector.tensor_copy(out=tabu[:], in_=posi[:]), reads=[bf("posi")], writes=[bf("tabu")])
            op(DVE, lambda: nc.vector.tensor_scalar(out=tabu[:], in0=tabu[:], scalar1=cols[:, 0:1], scalar2=cols[:, 1:2],
                                                    op0=ALU.mult, op1=ALU.add),
               reads=[bf("tabu"), bf("consts")], writes=[bf("tabu")])
            op(DVE, lambda: nc.vector.tensor_copy(out=tabT[:], in_=posi[:]), reads=[bf("posi")], writes=[bf("tabT")])
            op(DVE, lambda: nc.vector.scalar_tensor_tensor(out=tabu[:], in0=tabT[:], scalar=cols[:, 2:3], in1=tabu[:],
                                                           op0=ALU.mult, op1=ALU.add),
               reads=[bf("tabT"), bf("tabu"), bf("consts")], writes=[bf("tabu")])
            op(DVE, lambda: nc.vector.tensor_copy(out=posi[:], in_=tabu[:]), reads=[bf("tabu")], writes=[bf("posi")])
            op(DVE, lambda: nc.vector.tensor_copy(out=tabT[:], in_=posi[:]), reads=[bf("posi")], writes=[bf("tabT")])
            op(DVE, lambda: nc.vector.tensor_tensor(out=tabu[:], in0=tabu[:], in1=tabT[:], op=ALU.subtract),
               reads=[bf("tabT"), bf("tabu")], writes=[bf("tabu")])
            op(DVE, lambda: nc.vector.scalar_tensor_tensor(out=tabu[:], in0=tabu[:], scalar=0.5, in1=tabu[:],
                                                           op0=ALU.is_gt, op1=ALU.subtract),
               reads=[bf("tabu")], writes=[bf("tabu")])

        def table_sin():
            op(ACT, lambda: nc.scalar.activation(out=tabT[:], in_=tabu[:], func=AF.Sin, scale=-6.2831845),
               reads=[bf("tabu")], writes=[bf("tabT")])

        def prep_weights():
            dma(SP, d_st1, stg_uq, w_uq.rearrange("(kc p) c -> p kc c", p=128), writes=[bf("Vstage")])
            dma(SP, d_st2, stg_ukv, w_ukv.rearrange("(kc p) c -> p kc c", p=128), writes=[bf("Kstage")])
            dma(SP, d_st3, stg_pw, pool_w.rearrange("g c d -> c g d"), writes=[bf("kdstage")])
            load_slow_consts()
            dma(SP, d_ln, lng_bc[:], ln_g[0:1, :].broadcast_to([128, 1024]), writes=[bf("lnc")])
            dma(SP, d_ln, lnb_bc[:], ln_b[0:1, :].broadcast_to([128, 1024]), writes=[bf("lnc")])
            op(DVE, lambda: nc.vector.tensor_scalar(out=w_in_bf[:, :, C_ROT:C_ROT + 32], in0=w_in_bf[:, :, C_KR + 32:C_KR + 64],
                                                    scalar1=-1.0, scalar2=None, op0=ALU.mult),
               reads=[bf("w_in_kr")], writes=[bf("w_in_rot")])
            op(DVE, lambda: nc.vector.tensor_copy(out=w_in_bf[:, :, C_ROT + 32:C_ROT + 64], in_=w_in_bf[:, :, C_KR:C_KR + 32]),
               reads=[bf("w_in_kr")], writes=[bf("w_in_rot")])
            for kc in range(4):
                sv = stg_uq[:, kc, :].rearrange("p (h c) -> p h c", h=4)
                op(DVE, lambda kc=kc, sv=sv: nc.vector.tensor_scalar(out=w_uq_bf[:, kc, :, 0:192], in0=sv, scalar1=gq[:, kc:kc + 1],
                                                                      scalar2=None, op0=ALU.mult),
                   reads=[bf("Vstage"), bf("constg")], writes=[bf("w_uq")])
                op(DVE, lambda kc=kc, sv=sv: nc.vector.tensor_scalar(out=w_uq_bf[:, kc, :, 192:224], in0=sv[:, :, 160:192],
                                                                      scalar1=gq[:, kc:kc + 1], scalar2=-1.0, op0=ALU.mult, op1=ALU.mult),
                   reads=[bf("Vstage"), bf("constg")], writes=[bf("w_uq")])
                op(DVE, lambda kc=kc, sv=sv: nc.vector.tensor_scalar(out=w_uq_bf[:, kc, :, 224:256], in0=sv[:, :, 128:160],
                                                                      scalar1=gq[:, kc:kc + 1], scalar2=None, op0=ALU.mult),
                   reads=[bf("Vstage"), bf("constg")], writes=[bf("w_uq")])
            op(POOL, lambda: nc.gpsimd.memset(Vaug[:, :, :, 128:129], 1.0), writes=[bf("Vones"), bf("Vstage")])
            for kc in range(2):
                sv = stg_ukv[:, kc, :].rearrange("p (h c) -> p h c", h=4)
                op(DVE, lambda kc=kc, sv=sv: nc.vector.tensor_scalar(out=w_ukvk_bf[:, kc, :, :], in0=sv[:, :, 0:128], scalar1=gkv[:, kc:kc + 1],
                                                                      scalar2=None, op0=ALU.mult),
                   reads=[bf("Kstage"), bf("constg")], writes=[bf("w_ukv")])
                op(DVE, lambda kc=kc, sv=sv: nc.vector.tensor_scalar(out=w_ukvv_bf[:, kc, :].rearrange("p (h c) -> p h c", h=4),
                                                                      in0=sv[:, :, 128:256], scalar1=gkv[:, kc:kc + 1],
                                                                      scalar2=None, op0=ALU.mult),
                   reads=[bf("Kstage"), bf("constg")], writes=[bf("w_ukv")])
            for g in range(4):
                op(DVE, lambda g=g: nc.vector.tensor_scalar(out=pool_w_bf[:, g, :], in0=stg_pw[:, g, :], scalar1=1.0 / POOL_W[g],
                                                            scalar2=None, op0=ALU.mult),
                   reads=[bf("kdstage")], writes=[bf("pool_w")])


        def nxt(si, tg):
            return (si, tg + 1) if tg + 1 < NTG else (si + 1, 0)

        def emit_xT(si, tg, ring):
            for tt in range(4):
                for half in range(2):
                    bi = next_bank(ring)
                    pv = banks[bi][:].bitcast(BF16)[:, 0:512].rearrange("p (a b) -> p a b", a=4)
                    for j in range(4):
                        kc = half * 4 + j
                        op(PE, lambda pv=pv, j=j, kc=kc, tt=tt: nc.tensor.transpose(out=pv[:, j, :], in_=xbf[:, tt, kc * 128:(kc + 1) * 128],
                                                                                   identity=ident),
                           reads=[bf(f"xbf{tt}"), bf("mats")], writes=[bbuf[bi]], inc=(j == 3))
                    op(DVE, lambda pv=pv, half=half, tt=tt: nc.vector.tensor_copy(out=xT[:, half * 4:half * 4 + 4, tt * 128:(tt + 1) * 128], in_=pv),
                       reads=[bbuf[bi]], writes=[bf(f"xT{tt}_{half}")])
            n2 = nxt(si, tg)
            if n2[0] < nseq:
                load_xbf(*n2)

        for tt in range(4):
            dma(SP, xr_sem[tt], xres[:, tt, :], x[0, tt * 128:(tt + 1) * 128, :], writes=[bf(f"xres{tt}")])
        for tt in range(4):
            op(DVE, lambda tt=tt: nc.vector.tensor_copy(out=xbf[:, tt, :], in_=xres[:, tt, :]),
               reads=[bf(f"xres{tt}")], writes=[bf(f"xbf{tt}")])
        emit_xT(0, 0, ringA)
        build_table(0, 0)
        dma(POOL, dsem("w_out"), w_out_bf[:], w_out.rearrange("(kc p) c -> p kc c", p=128), writes=[bf("w_out")])
        prep_weights()

        def ln_c2_norm(si, tg, tt):
            xs = tt
            op(DVE, lambda: nc.vector.scalar_tensor_tensor(out=xres[:, xs, :], in0=xres[:, xs, :], scalar=mv[:, xs, 3:4], in1=lng_bc[:],
                                                           op0=ALU.add, op1=ALU.mult),
               reads=[bf(f"xres{xs}"), bf(f"mvc{xs}"), bf("lnc")], writes=[bf(f"xres{xs}")])

        def ln_c2_gb(si, tg, tt):
            t0 = tg * TG + tt * 128
            xs = tt
            op(DVE, lambda: nc.vector.scalar_tensor_tensor(out=xres[:, xs, :], in0=xres[:, xs, :], scalar=mv[:, xs, 2:3], in1=lnb_bc[:],
                                                           op0=ALU.mult, op1=ALU.add),
               reads=[bf(f"xres{xs}"), bf(f"mvb{xs}"), bf("lnc")], writes=[bf(f"xres{xs}")])
            dma(SP, st_sem[xs], out[si, t0:t0 + 128, :], xres[:, xs, :], reads=[bf(f"xres{xs}")])

        def ln_c2(si, tg, tt):
            ln_c2_norm(si, tg, tt)
            ln_c2_gb(si, tg, tt)

        tail_pending = []
        for si in range(nseq):
            for tg in range(NTG):
                tok0 = tg * TG
                nsi, ntg = (si, tg + 1) if tg + 1 < NTG else (si + 1, 0)
                last_tg = (si == nseq - 1 and tg == NTG - 1)

                def inproj_chunk(c0):
                    bi = next_bank(ringA)
                    for kc in range(8):
                        op(PE, lambda kc=kc, bi=bi: nc.tensor.matmul(banks[bi][:], lhsT=w_in_bf[:, kc, c0:c0 + 128], rhs=xT[:, kc, :],
                                                                     start=(kc == 0), stop=(kc == 7)),
                           reads=XT_ALL + SEG[c0], writes=[bbuf[bi]], inc=(kc == 7))
                    return bi

                for g in range(4):
                    w = POOL_W[g]
                    bi = inproj_chunk(C_U + g * 128)
                    op(ACT, lambda bi=bi, g=g: nc.scalar.activation(out=ubuf[:, g, HALO:HALO + TG], in_=banks[bi][:], func=AF.Copy),
                       reads=[bbuf[bi]], writes=[bf(f"u{g}")])
                    L = HALO + TG
                    src = ubuf[:, g, :]
                    nsteps = g + 1
                    sh = 1
                    lo = HALO - (w - 1)
                    cur_buf = bf(f"u{g}")
                    for st in range(nsteps):
                        dst = pws[:, st % 2, :]
                        dbuf = bf(f"pws{st % 2}")
                        lo2 = lo + sh
                        op(DVE, lambda dst=dst, src=src, lo2=lo2, sh=sh, L=L: nc.vector.tensor_tensor(
                            out=dst[:, lo2:L], in0=src[:, lo2:L], in1=src[:, lo2 - sh:L - sh], op=ALU.add),
                           reads=[cur_buf], writes=[dbuf])
                        src, cur_buf, lo, sh = dst, dbuf, lo2, sh * 2
                    if tg == 0:
                        op(DVE, lambda src=src, g=g: nc.vector.tensor_tensor(out=src[:, HALO:2 * HALO], in0=src[:, HALO:2 * HALO],
                                                                              in1=poolc[:, g, :], op=ALU.mult),
                           reads=[cur_buf, bf("consts")], writes=[cur_buf])
                    op(DVE, lambda g=g: nc.vector.tensor_copy(out=ubuf[:, g, 0:HALO], in_=ubuf[:, g, TG:TG + HALO]),
                       reads=[bf(f"u{g}")], writes=[bf(f"u{g}")])
                    op(DVE, lambda src=src, g=g, w=w: nc.vector.scalar_tensor_tensor(out=pooled_on[:, g, :], in0=ubuf[:, g, HALO:HALO + TG],
                                                                                    scalar=-float(w), in1=src[:, HALO:HALO + TG],
                                                                                    op0=ALU.mult, op1=ALU.add),
                       reads=[cur_buf, bf(f"u{g}")], writes=[bf(f"po{g}")])
                if tg == NTG - 1:
                    op(DVE, lambda: nc.vector.memset(ubuf[:, :, 0:HALO], 0.0), writes=[bf(f"u{g}") for g in range(4)])

                while tail_pending:
                    tail_pending.pop(0)()
                for c in range(4):
                    bi = inproj_chunk(C_GA + c * 128)
                    op(ACT, lambda bi=bi, c=c: nc.scalar.activation(out=sga[:, c, :], in_=banks[bi][:], func=AF.Silu),
                       reads=[bbuf[bi]], writes=[bf(f"sga{c}")])
                for c in range(4):
                    bi = inproj_chunk(C_GB + c * 128)
                    op(ACT, lambda bi=bi, c=c: nc.scalar.activation(out=sgb[:, c, :], in_=banks[bi][:], func=AF.Silu),
                       reads=[bbuf[bi]], writes=[bf(f"sgb{c}")])
                table_sin()
                op(ACT, lambda: nc.scalar.activation(out=smalls[:, 5:6], in_=smalls[:, 0:1], func=AF.Ln),
                   reads=[bf("smalls")], writes=[bf("dummy")])
                ssq_bank = {"q": 2, "kv": 3}
                SQ = [(sq_bf[:, 0, :], bf("sq0")), (sq_bf[:, 1, :], bf("sq1"))] + [(pT[:, i, :], bf(f"pT{i}")) for i in range(4)]
                colbank = 4
                sqi = 0
                pending = []

                def flush_pending():
                    while pending:
                        pending.pop(0)()

                def ssq_q_mm(sl, c):
                    op(PE, lambda: nc.tensor.matmul(banks[ssq_bank["q"]][:], lhsT=ones, rhs=SQ[sl][0],
                                                    start=(c == 0), stop=(c == 3)),
                       reads=[SQ[sl][1], bf("mats")], writes=[bbuf[ssq_bank["q"]]], inc=True)

                def ssq_kv_mm(sl, c):
                    op(PE, lambda: nc.tensor.matmul(banks[ssq_bank["kv"]][:], lhsT=ones, rhs=SQ[sl][0],
                                                    start=(c == 0), stop=(c == 1)),
                       reads=[SQ[sl][1], bf("mats")], writes=[bbuf[ssq_bank["kv"]]], inc=False)
                    for tt in range(4):
                        op(PE, lambda tt=tt: nc.tensor.matmul(banks[colbank][:, tt:tt + 1], lhsT=SQ[sl][0][:, tt * 128:(tt + 1) * 128],
                                                              rhs=ones[:, 0:1], start=(c == 0 and tt == 0), stop=(c == 1),
                                                              skip_group_check=True),
                           reads=[SQ[sl][1], bf("mats")], writes=[bbuf[colbank]], inc=(tt == 3))

                for c in range(4):
                    bi = inproj_chunk(C_XQ + c * 128)
                    flush_pending()
                    op(ACT, lambda bi=bi, c=c: nc.scalar.activation(out=xq_bf[:, c, :], in_=banks[bi][:], func=AF.Copy),
                       reads=[bbuf[bi]], writes=[bf(f"xq{c}")])
                    sl = sqi % 6
                    sqi += 1
                    op(ACT, lambda bi=bi, sl=sl: nc.scalar.activation(out=SQ[sl][0], in_=banks[bi][:], func=AF.Square),
                       reads=[bbuf[bi]], writes=[SQ[sl][1]])
                    pending.append(lambda sl=sl, c=c: ssq_q_mm(sl, c))
                for c in range(2):
                    bi = inproj_chunk(C_XKV + c * 128)
                    flush_pending()
                    op(ACT, lambda bi=bi, c=c: nc.scalar.activation(out=xkv_bf[:, c, :], in_=banks[bi][:], func=AF.Copy),
                       reads=[bbuf[bi]], writes=[bf(f"xkv{c}")])
                    sl = sqi % 6
                    sqi += 1
                    op(ACT, lambda bi=bi, sl=sl: nc.scalar.activation(out=SQ[sl][0], in_=banks[bi][:], func=AF.Square),
                       reads=[bbuf[bi]], writes=[SQ[sl][1]])
                    pending.append(lambda sl=sl, c=c: ssq_kv_mm(sl, c))
                for g in range(4):
                    bi = next_bank(ringA)
                    op(PE, lambda g=g, bi=bi: nc.tensor.matmul(banks[bi][:], lhsT=pool_w_bf[:, g, :], rhs=pooled_on[:, g, :], start=True, stop=True),
                       reads=[bf(f"po{g}"), bf("pool_w")], writes=[bbuf[bi]])
                    op(DVE, lambda g=g, bi=bi: nc.vector.scalar_tensor_tensor(out=yT[:, 4 + g, :], in0=banks[bi][:], scalar=pscale[:, g:g + 1],
                                                                            in1=sgb[:, g, :], op0=ALU.mult, op1=ALU.mult),
                       reads=[bbuf[bi], bf(f"sgb{g}"), bf("constg")], writes=[bf(f"yT{4 + g}")])

                bi_kr = inproj_chunk(C_KR)
                flush_pending()
                op(DVE, lambda bi=bi_kr: nc.vector.tensor_tensor(out=pws[:, 0, 0:TG], in0=banks[bi][:], in1=tabT[:], op=ALU.mult),
                   reads=[bbuf[bi_kr], bf("tabT")], writes=[bf("pws0")])
                op(DVE, lambda: nc.vector.tensor_copy(out=tabu[0:64, :], in_=pws[64:128, 0, 0:TG]), reads=[bf("pws0")], writes=[bf("tabu")])
                op(DVE, lambda: nc.vector.tensor_tensor(out=kdup[0:64, tok0:tok0 + TG], in0=pws[0:64, 0, 0:TG], in1=tabu[0:64, :], op=ALU.add),
                   reads=[bf("pws0"), bf("tabu")], writes=[bf("kdup"), bf("kdstage")])
                op(DVE, lambda: nc.vector.tensor_copy(out=kdup[64:128, tok0:tok0 + TG], in_=kdup[0:64, tok0:tok0 + TG]),
                   reads=[bf("kdup")], writes=[bf("kdup")])

                op(ACT, lambda: nc.scalar.activation(out=rinvq[:], in_=banks[ssq_bank["q"]][:], func=AF.Ln, bias=eps_rms, scale=1.0 / 512),
                   reads=[bbuf[ssq_bank["q"]], bf("smalls")], writes=[bf("rinvq")])
                op(ACT, lambda: nc.scalar.activation(out=rinvq[:], in_=rinvq[:], func=AF.Exp, scale=-0.5),
                   reads=[bf("rinvq")], writes=[bf("rinvq")])
                op(ACT, lambda: nc.scalar.activation(out=rinvkv[:], in_=banks[ssq_bank["kv"]][:], func=AF.Ln, bias=eps_rms, scale=1.0 / 256),
                   reads=[bbuf[ssq_bank["kv"]], bf("smalls")], writes=[bf("rinvkv")])
                op(ACT, lambda: nc.scalar.activation(out=rinvkv[:], in_=rinvkv[:], func=AF.Exp, scale=-0.5),
                   reads=[bf("rinvkv")], writes=[bf("rinvkv")])
                op(ACT, lambda: nc.scalar.activation(out=rinvcol[:], in_=banks[colbank][:, 0:4], func=AF.Ln, bias=eps_rms, scale=1.0 / 256),
                   reads=[bbuf[colbank], bf("smalls")], writes=[bf("rinvcol")])
                op(ACT, lambda: nc.scalar.activation(out=rinvcol[:], in_=rinvcol[:], func=AF.Exp, scale=-0.5),
                   reads=[bf("rinvcol")], writes=[bf("rinvcol")])
                op(DVE, lambda: nc.vector.tensor_tensor(out=tabT[:], in0=tabT[:], in1=rinvq[:], op=ALU.mult),
                   reads=[bf("tabT"), bf("rinvq")], writes=[bf("tabT")])

                XQ = [bf(f"xq{c}") for c in range(4)]
                XKV = [bf(f"xkv{c}") for c in range(2)]
                for h in range(4):
                    for part in range(2):
                        bi = next_bank(ringA)
                        c0 = 0 if part == 0 else 128
                        for kc in range(4):
                            op(PE, lambda kc=kc, bi=bi, h=h, c0=c0: nc.tensor.matmul(banks[bi][:], lhsT=w_uq_bf[:, kc, h, c0:c0 + 128], rhs=xq_bf[:, kc, :],
                                                                                    start=(kc == 0), stop=(kc == 3)),
                               reads=XQ + [bf("w_uq")], writes=[bbuf[bi]], inc=(kc == 3))
                        if part == 0:
                            op(DVE, lambda bi=bi, h=h: nc.vector.tensor_tensor(out=qn[:, h, :], in0=banks[bi][:], in1=rinvq[:], op=ALU.mult),
                               reads=[bbuf[bi], bf("rinvq")], writes=[bf(f"qn{h}")])
                        else:
                            op(DVE, lambda bi=bi, h=h: nc.vector.tensor_tensor(out=qr[:, h, :], in0=banks[bi][:], in1=tabT[:], op=ALU.mult),
                               reads=[bbuf[bi], bf("tabT")], writes=[bf(f"qr{h}")])
                for h in range(4):
                    bi = next_bank(ringA)
                    for kc in range(2):
                        op(PE, lambda kc=kc, bi=bi, h=h: nc.tensor.matmul(banks[bi][:], lhsT=w_ukvk_bf[:, kc, h, :], rhs=xkv_bf[:, kc, :],
                                                                         start=(kc == 0), stop=(kc == 1)),
                           reads=XKV + [bf("w_ukv")], writes=[bbuf[bi]], inc=(kc == 1))
                    op(DVE, lambda bi=bi, h=h: nc.vector.tensor_tensor(out=Kn[:, h, tok0:tok0 + TG], in0=banks[bi][:], in1=rinvkv[:], op=ALU.mult),
                       reads=[bbuf[bi], bf("rinvkv")], writes=[bf(f"Kn{h}"), bf("Kstage")])
                for tt in range(4):
                    bi = next_bank(ringA)
                    kb = tg * 4 + tt
                    for kc in range(2):
                        op(PE, lambda kc=kc, bi=bi, tt=tt: nc.tensor.matmul(banks[bi][:], lhsT=xkv_bf[:, kc, tt * 128:(tt + 1) * 128], rhs=w_ukvv_bf[:, kc, :],
                                                                           start=(kc == 0), stop=(kc == 1)),
                           reads=XKV + [bf("w_ukv")], writes=[bbuf[bi]], inc=(kc == 1))
                    if tt < 2:
                        op(ACT, lambda bi=bi, tt=tt, kb=kb: nc.scalar.activation(out=Vaug[:, kb, :, 0:128],
                                                                                in_=banks[bi][:].rearrange("p (h c) -> p h c", h=4),
                                                                                func=AF.Identity, scale=rinvcol[:, tt:tt + 1]),
                           reads=[bbuf[bi], bf("rinvcol")], writes=[bf("V"), bf("Vstage")])
                    else:
                        op(DVE, lambda bi=bi, tt=tt, kb=kb: nc.vector.tensor_scalar(out=Vaug[:, kb, :, 0:128],
                                                                                   in0=banks[bi][:].rearrange("p (h c) -> p h c", h=4),
                                                                                   scalar1=rinvcol[:, tt:tt + 1], scalar2=None, op0=ALU.mult),
                           reads=[bbuf[bi], bf("rinvcol")], writes=[bf("V"), bf("Vstage")])

                for tt in range(4):
                    t0 = tok0 + tt * 128
                    dma(SP, xr_sem[tt], xres[:, tt, :], x[si, t0:t0 + 128, :], writes=[bf(f"xres{tt}")])
                if nsi < nseq:
                    build_table(nsi, ntg)
                nkb = 4 * (tg + 1)
                steps = []

                def emit_pv(h, kb, slot, part=None):
                    pvset = (4, 5) if h % 2 == 0 else (6, 7)
                    j = kb - 4 * tg
                    qls = list(range(max(j, 0), 4))
                    if part == 0:
                        qls = qls[:len(qls) // 2]
                    elif part == 1:
                        qls = qls[len(qls) // 2:]
                    for ql in qls:
                        pb = pvset[ql // 2]
                        gcol = (ql % 2) * 129
                        op(PE, lambda ql=ql, pb=pb, gcol=gcol: nc.tensor.matmul(
                            banks[pb][:, gcol:gcol + 129], lhsT=pT[:, slot, ql * 128:(ql + 1) * 128], rhs=Vaug[:, kb, h, :],
                            start=(kb == 0 and ql % 2 == 0), stop=(kb == 4 * tg + ql), skip_group_check=True),
                           reads=[bf(f"pT{slot}"), bf("V"), bf("Vones")], writes=[bbuf[pb]], inc=(ql == 3))
                    if j in (1, 3) and part != 0:
                        half = j // 2
                        pb = pvset[half]
                        pv3 = banks[pb][:, 0:258].rearrange("p (g c) -> p g c", g=2)
                        op(DVE, lambda: nc.vector.reciprocal(out=rs[:, h, :], in_=pv3[:, :, 128]),
                           reads=[bbuf[pb]], writes=[bf(f"rs{h}")])
                        op(DVE, lambda: nc.vector.tensor_tensor(
                            out=pooled_on[:, 2 * half:2 * half + 2, h * 128:(h + 1) * 128], in0=pv3[:, :, 0:128],
                            in1=rs[:, h, :].unsqueeze(2).to_broadcast([128, 2, 128]), op=ALU.mult),
                           reads=[bbuf[pb], bf(f"rs{h}")], writes=[bf(f"po{2 * half}"), bf(f"po{2 * half + 1}")])

                for h in range(4):
                    for kb in range(nkb):
                        j = kb - 4 * tg
                        c0 = 128 * j if j > 0 else 0
                        bi = next_bank(ringS)
                        slot = state["pt"] % 4
                        state["pt"] += 1
                        op(PE, lambda bi=bi, kb=kb, c0=c0, h=h: nc.tensor.matmul(banks[bi][:, c0:TG], lhsT=Kn[:, h, kb * 128:(kb + 1) * 128], rhs=qn[:, h, c0:TG],
                                                                                start=True, stop=False),
                           reads=[bf(f"Kn{h}"), bf(f"qn{h}")], writes=[bbuf[bi]], inc=False)
                        st = steps.pop(0) if len(steps) >= 3 else None
                        if st is not None:
                            emit_pv(*st, part=0)
                        op(PE, lambda bi=bi, kb=kb, c0=c0, h=h, j=j: nc.tensor.matmul(banks[bi][:, c0:TG], lhsT=kdup[:, kb * 128:(kb + 1) * 128], rhs=qr[:, h, c0:TG],
                                                                                     start=False, stop=(j < 0)),
                           reads=[bf("kdup"), bf(f"qr{h}")], writes=[bbuf[bi]], inc=(j < 0))
                        if j >= 0:
                            op(PE, lambda bi=bi, c0=c0: nc.tensor.matmul(banks[bi][:, c0:c0 + 128], lhsT=ident, rhs=negmask, start=False, stop=True),
                               reads=[bf("mats")], writes=[bbuf[bi]], inc=True)
                        op(ACT, lambda bi=bi, slot=slot, c0=c0: nc.scalar.activation(out=pT[:, slot, c0:TG], in_=banks[bi][:, c0:TG], func=AF.Exp,
                                                                                    bias=zero_col, scale=SCALE),
                           reads=[bbuf[bi], bf("smalls")], writes=[bf(f"pT{slot}")])
                        if st is not None:
                            emit_pv(*st, part=1)
                        steps.append((h, kb, slot))
                while steps:
                    emit_pv(*steps.pop(0))
                if nsi < nseq:
                    emit_xT(nsi, ntg, ringS)

                for tt in range(4):
                    bi = next_bank(ringC)
                    pv = banks[bi][:].bitcast(BF16)[:, 0:512].rearrange("p (a b) -> p a b", a=4)
                    for h in range(4):
                        op(PE, lambda pv=pv, h=h, tt=tt: nc.tensor.transpose(out=pv[:, h, :], in_=pooled_on[:, tt, h * 128:(h + 1) * 128], identity=ident),
                           reads=[bf(f"po{tt}"), bf("mats")], writes=[bbuf[bi]], inc=(h == 3))
                    op(DVE, lambda pv=pv, tt=tt: nc.vector.tensor_tensor(out=yT[:, 0:4, tt * 128:(tt + 1) * 128], in0=pv,
                                                                        in1=sga[:, :, tt * 128:(tt + 1) * 128], op=ALU.mult),
                       reads=[bbuf[bi]] + [bf(f"sga{c}") for c in range(4)], writes=[bf(f"yTa{tt}")])
                for tt in range(4):
                    t0 = tok0 + tt * 128
                    xs = tt
                    if last_tg and tt > 0:
                        ln_c2_norm(si, tg, tt - 1)
                    YDEPS = [bf(f"yTa{tt}")] + [bf(f"yT{4 + g}") for g in range(4)] + [bf("w_out")]
                    for half in range(2):
                        bi = next_bank(ringC)
                        for c in range(8):
                            op(PE, lambda c=c, bi=bi, half=half, tt=tt: nc.tensor.matmul(banks[bi][:], lhsT=yT[:, c, tt * 128:(tt + 1) * 128],
                                                                                        rhs=w_out_bf[:, c, half * 512:(half + 1) * 512],
                                                                                        start=(c == 0), stop=(c == 7)),
                               reads=YDEPS, writes=[bbuf[bi]], inc=(c == 7))
                        op(DVE, lambda bi=bi, half=half, xs=xs: nc.vector.scalar_tensor_tensor(
                            out=xres[:, xs, half * 512:(half + 1) * 512], in0=xres[:, xs, half * 512:(half + 1) * 512], scalar=ALPHA,
                            in1=banks[bi][:], op0=ALU.mult, op1=ALU.add),
                           reads=[bbuf[bi], bf(f"xres{xs}")], writes=[bf(f"xres{xs}")])
                        op(DVE, lambda half=half, xs=xs: nc.vector.bn_stats(out=stats[:, xs, half * 6:(half + 1) * 6],
                                                                           in_=xres[:, xs, half * 512:(half + 1) * 512]),
                           reads=[bf(f"xres{xs}")], writes=[bf(f"stats{xs}")])
                    op(DVE, lambda xs=xs: nc.vector.bn_aggr(out=mv[:, xs, 0:2], in_=stats[:, xs, :]),
                       reads=[bf(f"stats{xs}")], writes=[bf(f"mv{xs}")])
                    op(DVE, lambda xs=xs: nc.vector.tensor_scalar(out=mv[:, xs, 3:4], in0=mv[:, xs, 0:1], scalar1=-1.0, scalar2=None, op0=ALU.mult),
                       reads=[bf(f"mv{xs}")], writes=[bf(f"mvc{xs}")])
                    op(ACT, lambda xs=xs: nc.scalar.activation(out=mv[:, xs, 2:3], in_=mv[:, xs, 1:2], func=AF.Ln, bias=eps_ln, scale=1.0),
                       reads=[bf(f"mv{xs}"), bf("smalls")], writes=[bf(f"mvb{xs}")])
                    op(ACT, lambda xs=xs: nc.scalar.activation(out=mv[:, xs, 2:3], in_=mv[:, xs, 2:3], func=AF.Exp, scale=-0.5),
                       reads=[bf(f"mvb{xs}")], writes=[bf(f"mvb{xs}")])
                    if last_tg and tt > 0:
                        ln_c2_gb(si, tg, tt - 1)
                if last_tg:
                    ln_c2(si, tg, 3)
                else:
                    op(ACT, lambda: nc.scalar.activation(out=smalls[:, 4:5], in_=smalls[:, 2:3], func=AF.Silu),
                       reads=[bf("smalls")], writes=[bf("dummy")])
                    for tt in range(4):
                        ln_c2(si, tg, tt)

        while tail_pending:
            tail_pending.pop(0)()
        for xs in range(4):
            nc.sync.wait_ge(st_sem[xs].sem, st_sem[xs].n)
    return nc


def _consts():
    half = 32
    inv_freq = (10000.0 ** (-np.arange(half, dtype=np.float64) / half)) / (2.0 * np.pi)
    cols = np.zeros((128, 8), np.float32)
    for p in range(128):
        hi = np.float32(inv_freq[p % 32])
        cols[p, 0] = hi
        cols[p, 2] = np.float32(inv_freq[p % 32] - np.float64(hi))
        cols[p, 1] = 0.25 if p < 64 else 0.0
    pc = np.ones((128, 4, HALO), np.float32)
    for g, w in enumerate(POOL_W):
        for t in range(HALO):
            pc[:, g, t] = w / min(t + 1, w)
    mats = np.zeros((128, 3, 128), np.float32)
    mats[:, 0, :] = np.eye(128, dtype=np.float32)
    k = np.arange(128)[:, None]
    q = np.arange(128)[None, :]
    mats[:, 1, :] = np.where(k > q, NEG, 0.0)
    mats[:, 2, :] = 1.0
    return cols, pc.reshape(128, 4 * HALO), mats


_NC_CACHE = {}


def run(inputs, nseq, ncores):
    if nseq not in _NC_CACHE:
        _NC_CACHE[nseq] = build_nc(nseq)
    nc = _NC_CACHE[nseq]
    cols, pc, mats = _consts()
    x = np.ascontiguousarray(np.asarray(inputs["x"], dtype=np.float32))
    pos = np.ascontiguousarray(np.asarray(inputs["positions"], dtype=np.int32))
    shared = {k: np.ascontiguousarray(np.asarray(inputs[k], dtype=np.float32)) for k in
              ("w_in", "q_norm_g", "w_uq", "kv_norm_g", "w_ukv", "pool_w", "pool_scale", "w_out", "ln_g", "ln_b")}
    shared.update({"cst_cols": cols, "pool_c": pc, "cst_mats": mats})
    in_maps = []
    for i in range(ncores):
        m = dict(shared)
        m["x"] = x[i * nseq:(i + 1) * nseq]
        m["positions"] = pos[i * nseq:(i + 1) * nseq]
        in_maps.append(m)
    res = run_bass_kernel_spmd(nc, in_maps, core_ids=list(range(ncores)))
    return np.concatenate([r["out"] for r in res.results], axis=0)


def kernel(x, positions, w_in, q_norm_g, w_uq, kv_norm_g, w_ukv, pool_w, pool_scale, w_out, ln_g, ln_b):
    inputs = dict(x=x, positions=positions, w_in=w_in, q_norm_g=q_norm_g, w_uq=w_uq, kv_norm_g=kv_norm_g, w_ukv=w_ukv,
                  pool_w=pool_w, pool_scale=pool_scale, w_out=w_out, ln_g=ln_g, ln_b=ln_b)
    return run(inputs, NSEQ, NCORES).astype(np.float32)
```

```python
import numpy as np
from contextlib import ExitStack
import concourse.bass as bass
import concourse.mybir as mybir
from concourse.bass_utils import run_bass_kernel_spmd

F32 = mybir.dt.float32
BF16 = mybir.dt.bfloat16
I32 = mybir.dt.int32
ALU = mybir.AluOpType
AF = mybir.ActivationFunctionType

NCORES = 8
BATCH = 32
S = 2048
D = 1024
NSEQ = BATCH // NCORES
TG = 512
NTG = S // TG
SCALE = 192.0 ** -0.5
ALPHA = 2.0 ** 0.25
RMS_EPS = 1e-6
LN_EPS = 1e-5
NEG = -30000.0
POOL_W = (2, 4, 8, 16)
HALO = 16
C_GA, C_GB, C_U, C_XQ, C_XKV, C_KR, C_ROT = 0, 512, 1024, 1536, 2048, 2304, 2368
WIN_COLS = 2432


class Tok:
    __slots__ = ("sem", "key", "val", "eng")

    def __init__(self, sem, key, val, eng):
        self.sem, self.key, self.val, self.eng = sem, key, val, eng


class Buf:
    __slots__ = ("name", "w", "r")

    def __init__(self, name):
        self.name, self.w, self.r = name, None, {}


class Eng:
    def __init__(self, name, h, sem, is_pe=False):
        self.name, self.h, self.sem, self.is_pe = name, h, sem, is_pe
        self.n = 0
        self.seen = {}


class DmaSem:
    def __init__(self, name, sem):
        self.name, self.sem, self.n = name, sem, 0


def _wait_deps(E, reads, writes):
    deps = []
    for b in reads:
        if b.w is not None:
            deps.append(b.w)
    for b in writes:
        if b.w is not None:
            deps.append(b.w)
        deps.extend(b.r.values())
    for t in deps:
        if t.eng is E and E.is_pe:
            continue
        if E.seen.get(t.key, 0) >= t.val:
            continue
        E.h.wait_ge(t.sem, t.val)
        E.seen[t.key] = t.val


def _record(tok, reads, writes):
    for b in reads:
        o = b.r.get(tok.key)
        if o is None or o.val < tok.val:
            b.r[tok.key] = tok
    for b in writes:
        b.w = tok
        b.r = {}


def op(E, fn, reads=(), writes=(), inc=True):
    _wait_deps(E, reads, writes)
    ins = fn()
    if inc:
        E.n += 1
        ins.then_inc(E.sem, 1)
        tok = Tok(E.sem, E.name, E.n, E)
    else:
        tok = Tok(E.sem, E.name, E.n + 1, E)
    _record(tok, reads, writes)
    return ins


def dma(Q, ds, out_ap, in_ap, reads=(), writes=(), **kw):
    _wait_deps(Q, reads, writes)
    ins = Q.h.dma_start(out=out_ap, in_=in_ap, **kw)
    ds.n += 16
    ins.then_inc(ds.sem, 16)
    tok = Tok(ds.sem, ds.name, ds.n, None)
    _record(tok, reads, writes)
    return ins


def build_nc(nseq=NSEQ):
    nc = bass.Bass("TRN2", target_bir_lowering=False)

    def din(name, shape, dt=F32):
        return nc.dram_tensor(name, list(shape), dt, kind="ExternalInput").ap()

    x = din("x", [nseq, S, D])
    positions = din("positions", [nseq, S], I32)
    w_in = din("w_in", [1024, 2368])
    q_norm_g = din("q_norm_g", [512])
    w_uq = din("w_uq", [512, 768])
    kv_norm_g = din("kv_norm_g", [256])
    w_ukv = din("w_ukv", [256, 1024])
    pool_w = din("pool_w", [4, 128, 128])
    pool_scale = din("pool_scale", [512])
    w_out = din("w_out", [1024, 1024])
    ln_g = din("ln_g", [1, 1024])
    ln_b = din("ln_b", [1, 1024])
    cst_cols = din("cst_cols", [128, 8])
    pool_c = din("pool_c", [128, 4 * HALO])
    cst_mats = din("cst_mats", [128, 3, 128])
    out = nc.dram_tensor("out", [nseq, S, D], F32, kind="ExternalOutput").ap()

    with ExitStack() as es:
        def sb(name, shape, dt):
            return es.enter_context(nc.sbuf_tensor(name, list(shape), dt))

        def newsem(name):
            return es.enter_context(nc.semaphore(name))

        PE = Eng("pe", nc.tensor, newsem("s_pe"), is_pe=True)
        ACT = Eng("act", nc.scalar, newsem("s_act"))
        DVE = Eng("dve", nc.vector, newsem("s_dve"))
        POOL = Eng("pool", nc.gpsimd, newsem("s_pool"))
        SP = Eng("sp", nc.sync, newsem("s_sp"))

        def dsem(name):
            return DmaSem(name, newsem("d_" + name))

        banks = [es.enter_context(nc.psum_tensor(f"pb{i}", [128, 512], F32)) for i in range(8)]
        bbuf = [Buf(f"bank{i}") for i in range(8)]

        w_in_bf = sb("w_in_bf", [128, 8, WIN_COLS], BF16)
        w_uq_bf = sb("w_uq_bf", [128, 4, 4, 256], BF16)
        w_ukvk_bf = sb("w_ukvk_bf", [128, 2, 4, 128], BF16)
        w_ukvv_bf = sb("w_ukvv_bf", [128, 2, 512], BF16)
        pool_w_bf = sb("pool_w_bf", [128, 4, 128], BF16)
        w_out_bf = sb("w_out_bf", [128, 8, 1024], BF16)
        mats_bf = sb("mats_bf", [128, 3, 128], BF16)
        lng_bc = sb("lng_bc", [128, 1024], F32)
        lnb_bc = sb("lnb_bc", [128, 1024], F32)
        cols = sb("cols", [128, 8], F32)
        gq = sb("gq", [128, 4], F32)
        gkv = sb("gkv", [128, 2], F32)
        pscale = sb("pscale", [128, 4], F32)
        poolc = sb("poolc", [128, 4, HALO], F32)
        smalls = sb("smalls", [128, 8], F32)
        Kn_flat = sb("Kn", [128, 4 * S], BF16)
        Kn = Kn_flat[:].rearrange("p (a b) -> p a b", a=4)
        kdup = sb("kdup", [128, S], BF16)
        Vaug_flat = sb("Vaug", [128, 16 * 4 * 129], BF16)
        Vaug = Vaug_flat[:].rearrange("p (a b c) -> p a b c", a=16, b=4)
        xbf = sb("xbf", [128, 4, 1024], BF16)
        xT = sb("xT", [128, 8, TG], BF16)
        xq_bf = sb("xq_bf", [128, 4, TG], BF16)
        xkv_bf = sb("xkv_bf", [128, 2, TG], BF16)
        sq_bf = sb("sq_bf", [128, 2, TG], BF16)
        rinvq = sb("rinvq", [128, TG], F32)
        rinvkv = sb("rinvkv", [128, TG], F32)
        rinvcol = sb("rinvcol", [128, 4], F32)
        tabT = sb("tabT", [128, TG], F32)
        tabu = sb("tabu", [128, TG], F32)
        qn = sb("qn", [128, 4, TG], BF16)
        qr = sb("qr", [128, 4, TG], BF16)
        sga = sb("sga", [128, 4, TG], BF16)
        sgb = sb("sgb", [128, 4, TG], BF16)
        ubuf = sb("ubuf", [128, 4, HALO + TG], F32)
        pws = sb("pws", [128, 2, HALO + TG], F32)
        pooled_on = sb("pooled_on", [128, 4, TG], BF16)
        posi = sb("posi", [128, TG], I32)
        pT = sb("pT", [128, 4, TG], BF16)
        rs = sb("rs", [128, 4, 2], F32)
        yT = sb("yT", [128, 8, TG], BF16)
        xres = sb("xres", [128, 4, 1024], F32)
        stats = sb("stats", [128, 4, 12], F32)
        mv = sb("mv", [128, 4, 4], F32)

        ident = mats_bf[:, 0, :]
        negmask = mats_bf[:, 1, :]
        ones = mats_bf[:, 2, :]
        eps_rms = smalls[:, 0:1]
        eps_ln = smalls[:, 1:2]
        zero_col = smalls[:, 2:3]

        B = {}

        def bf(name):
            if name not in B:
                B[name] = Buf(name)
            return B[name]

        op(POOL, lambda: nc.gpsimd.memset(smalls[:, 0:1], RMS_EPS), writes=[bf("smalls")])
        op(POOL, lambda: nc.gpsimd.memset(smalls[:, 1:2], LN_EPS), writes=[bf("smalls")])
        op(POOL, lambda: nc.gpsimd.memset(smalls[:, 2:3], 0.0), writes=[bf("smalls")])
        op(POOL, lambda: nc.gpsimd.memset(ubuf[:, :, 0:HALO], 0.0), writes=[bf(f"u{g}") for g in range(4)])

        d_c = dsem("consts")
        dma(SP, d_c, cols[:], cst_cols, writes=[bf("consts")])
        dma(SP, d_c, poolc[:].rearrange("p g t -> p (g t)"), pool_c, writes=[bf("consts")])
        d_ln = dsem("lnc")
        d_g = dsem("constg")

        def load_slow_consts():
            dma(SP, d_g, gq[:], q_norm_g.rearrange("(kc p) -> p kc", p=128), writes=[bf("constg")],
                allow_slow_non_contiguous=True)
            dma(SP, d_g, gkv[:], kv_norm_g.rearrange("(kc p) -> p kc", p=128), writes=[bf("constg")],
                allow_slow_non_contiguous=True)
            dma(SP, d_g, pscale[:], pool_scale.rearrange("(g p) -> p g", p=128), writes=[bf("constg")],
                allow_slow_non_contiguous=True)

        d_m = dsem("mats")
        dma(POOL, d_m, mats_bf[:], cst_mats, writes=[bf("mats")])

        xb_sem = [dsem(f"xbf{i}") for i in range(4)]
        xr_sem = [dsem(f"xres{i}") for i in range(4)]
        pos_sem = dsem("pos")
        st_sem = [dsem(f"st{i}") for i in range(4)]

        def load_xbf(si, tg):
            for tt in range(4):
                t0 = tg * TG + tt * 128
                dma(POOL, xb_sem[tt], xbf[:, tt, :], x[si, t0:t0 + 128, :], writes=[bf(f"xbf{tt}")])

        w_in_v = w_in.rearrange("(kc p) c -> p kc c", p=128)
        SEG = {}
        for (nm, dst, src, n) in (("kr", C_KR, 768, 64), ("u", C_U, 1344, 512), ("ga", C_GA, 832, 512), ("gb", C_GB, 1856, 512),
                                  ("xq", C_XQ, 0, 512), ("xkv", C_XKV, 512, 256)):
            if nm == "u":
                for i in range(4):
                    dma(POOL, dsem(f"w_in_u{i}"), w_in_bf[:, :, dst + 128 * i:dst + 128 * (i + 1)],
                        w_in_v[:, :, src + 128 * i:src + 128 * (i + 1)], writes=[bf(f"w_in_u{i}")])
                    SEG[dst + 128 * i] = [bf(f"w_in_u{i}")]
                continue
            dma(POOL, dsem("w_in_" + nm), w_in_bf[:, :, dst:dst + n], w_in_v[:, :, src:src + n], writes=[bf("w_in_" + nm)])
            for c in range(dst, dst + n, 128):
                SEG[c] = [bf("w_in_" + nm)]
        SEG[C_KR] = [bf("w_in_kr"), bf("w_in_rot")]

        d_st1, d_st2, d_st3 = dsem("stage1"), dsem("stage2"), dsem("stage3")
        stg_uq = Vaug_flat[:, 0:4 * 768 * 2].bitcast(F32).rearrange("p (k c) -> p k c", k=4)
        stg_ukv = Kn_flat[:, 0:2 * 1024 * 2].bitcast(F32).rearrange("p (k c) -> p k c", k=2)
        stg_pw = kdup[:, 0:1024].bitcast(F32).rearrange("p (g c) -> p g c", g=4)
        state = {"bank": 0, "pt": 0}

        def next_bank(ring):
            i = ring[state["bank"] % len(ring)]
            state["bank"] += 1
            return i

        ringA = [0, 1, 5, 6, 7]
        ringC = [0, 1, 2, 3, 4, 5, 6, 7]
        ringS = [0, 1, 2, 3]

        XT_ALL = [bf(f"xT{tt}_{half}") for tt in range(4) for half in range(2)]

        def build_table(si, tg):
            tok0 = tg * TG
            dma(SP, pos_sem, posi[:], positions[si:si + 1, tok0:tok0 + TG].broadcast_to([128, TG]), writes=[bf("posi")])
            op(DVE, lambda: nc.vector.tensor_copy(out=tabu[:], in_=posi[:]), reads=[bf("posi")], writes=[bf("tabu")])
            op(DVE, lambda: nc.vector.tensor_scalar(out=tabu[:], in0=tabu[:], scalar1=cols[:, 0:1], scalar2=cols[:, 1:2],
                                                    op0=ALU.mult, op1=ALU.add),
               reads=[bf("tabu"), bf("consts")], writes=[bf("tabu")])
            op(DVE, lambda: nc.vector.tensor_copy(out=tabT[:], in_=posi[:]), reads=[bf("posi")], writes=[bf("tabT")])
            op(DVE, lambda: nc.vector.scalar_tensor_tensor(out=tabu[:], in0=tabT[:], scalar=cols[:, 2:3], in1=tabu[:],
                                                           op0=ALU.mult, op1=ALU.add),
               reads=[bf("tabT"), bf("tabu"), bf("consts")], writes=[bf("tabu")])
            op(DVE, lambda: nc.vector.tensor_copy(out=posi[:], in_=tabu[:]), reads=[bf("tabu")], writes=[bf("posi")])
            op(DVE, lambda: nc.vector.tensor_copy(out=tabT[:], in_=posi[:]), reads=[bf("posi")], writes=[bf("tabT")])
            op(DVE, lambda: nc.vector.tensor_tensor(out=tabu[:], in0=tabu[:], in1=tabT[:], op=ALU.subtract),
               reads=[bf("tabT"), bf("tabu")], writes=[bf("tabu")])
            op(DVE, lambda: nc.vector.scalar_tensor_tensor(out=tabu[:], in0=tabu[:], scalar=0.5, in1=tabu[:],
                                                           op0=ALU.is_gt, op1=ALU.subtract),
               reads=[bf("tabu")], writes=[bf("tabu")])

        def table_sin():
            op(ACT, lambda: nc.scalar.activation(out=tabT[:], in_=tabu[:], func=AF.Sin, scale=-6.2831845),
               reads=[bf("tabu")], writes=[bf("tabT")])

        def prep_weights():
            dma(SP, d_st1, stg_uq, w_uq.rearrange("(kc p) c -> p kc c", p=128), writes=[bf("Vstage")])
            dma(SP, d_st2, stg_ukv, w_ukv.rearrange("(kc p) c -> p kc c", p=128), writes=[bf("Kstage")])
            dma(SP, d_st3, stg_pw, pool_w.rearrange("g c d -> c g d"), writes=[bf("kdstage")])
            load_slow_consts()
            dma(SP, d_ln, lng_bc[:], ln_g[0:1, :].broadcast_to([128, 1024]), writes=[bf("lnc")])
            dma(SP, d_ln, lnb_bc[:], ln_b[0:1, :].broadcast_to([128, 1024]), writes=[bf("lnc")])
            op(DVE, lambda: nc.vector.tensor_scalar(out=w_in_bf[:, :, C_ROT:C_ROT + 32], in0=w_in_bf[:, :, C_KR + 32:C_KR + 64],
                                                    scalar1=-1.0, scalar2=None, op0=ALU.mult),
               reads=[bf("w_in_kr")], writes=[bf("w_in_rot")])
            op(DVE, lambda: nc.vector.tensor_copy(out=w_in_bf[:, :, C_ROT + 32:C_ROT + 64], in_=w_in_bf[:, :, C_KR:C_KR + 32]),
               reads=[bf("w_in_kr")], writes=[bf("w_in_rot")])
            for kc in range(4):
                sv = stg_uq[:, kc, :].rearrange("p (h c) -> p h c", h=4)
                op(DVE, lambda kc=kc, sv=sv: nc.vector.tensor_scalar(out=w_uq_bf[:, kc, :, 0:192], in0=sv, scalar1=gq[:, kc:kc + 1],
                                                                      scalar2=None, op0=ALU.mult),
                   reads=[bf("Vstage"), bf("constg")], writes=[bf("w_uq")])
                op(DVE, lambda kc=kc, sv=sv: nc.vector.tensor_scalar(out=w_uq_bf[:, kc, :, 192:224], in0=sv[:, :, 160:192],
                                                                      scalar1=gq[:, kc:kc + 1], scalar2=-1.0, op0=ALU.mult, op1=ALU.mult),
                   reads=[bf("Vstage"), bf("constg")], writes=[bf("w_uq")])
                op(DVE, lambda kc=kc, sv=sv: nc.vector.tensor_scalar(out=w_uq_bf[:, kc, :, 224:256], in0=sv[:, :, 128:160],
                                                                      scalar1=gq[:, kc:kc + 1], scalar2=None, op0=ALU.mult),
                   reads=[bf("Vstage"), bf("constg")], writes=[bf("w_uq")])
            op(POOL, lambda: nc.gpsimd.memset(Vaug[:, :, :, 128:129], 1.0), writes=[bf("Vones"), bf("Vstage")])
            for kc in range(2):
                sv = stg_ukv[:, kc, :].rearrange("p (h c) -> p h c", h=4)
                op(DVE, lambda kc=kc, sv=sv: nc.vector.tensor_scalar(out=w_ukvk_bf[:, kc, :, :], in0=sv[:, :, 0:128], scalar1=gkv[:, kc:kc + 1],
                                                                      scalar2=None, op0=ALU.mult),
                   reads=[bf("Kstage"), bf("constg")], writes=[bf("w_ukv")])
                op(DVE, lambda kc=kc, sv=sv: nc.vector.tensor_scalar(out=w_ukvv_bf[:, kc, :].rearrange("p (h c) -> p h c", h=4),
                                                                      in0=sv[:, :, 128:256], scalar1=gkv[:, kc:kc + 1],
                                                                      scalar2=None, op0=ALU.mult),
                   reads=[bf("Kstage"), bf("constg")], writes=[bf("w_ukv")])
            for g in range(4):
                op(DVE, lambda g=g: nc.vector.tensor_scalar(out=pool_w_bf[:, g, :], in0=stg_pw[:, g, :], scalar1=1.0 / POOL_W[g],
                                                            scalar2=None, op0=ALU.mult),
                   reads=[bf("kdstage")], writes=[bf("pool_w")])


        def nxt(si, tg):
            return (si, tg + 1) if tg + 1 < NTG else (si + 1, 0)

        def emit_xT(si, tg, ring):
            for tt in range(4):
                for half in range(2):
                    bi = next_bank(ring)
                    pv = banks[bi][:].bitcast(BF16)[:, 0:512].rearrange("p (a b) -> p a b", a=4)
                    for j in range(4):
                        kc = half * 4 + j
                        op(PE, lambda pv=pv, j=j, kc=kc, tt=tt: nc.tensor.transpose(out=pv[:, j, :], in_=xbf[:, tt, kc * 128:(kc + 1) * 128],
                                                                                   identity=ident),
                           reads=[bf(f"xbf{tt}"), bf("mats")], writes=[bbuf[bi]], inc=(j == 3))
                    op(DVE, lambda pv=pv, half=half, tt=tt: nc.vector.tensor_copy(out=xT[:, half * 4:half * 4 + 4, tt * 128:(tt + 1) * 128], in_=pv),
                       reads=[bbuf[bi]], writes=[bf(f"xT{tt}_{half}")])
            n2 = nxt(si, tg)
            if n2[0] < nseq:
                load_xbf(*n2)

        for tt in range(4):
            dma(SP, xr_sem[tt], xres[:, tt, :], x[0, tt * 128:(tt + 1) * 128, :], writes=[bf(f"xres{tt}")])
        for tt in range(4):
            op(DVE, lambda tt=tt: nc.vector.tensor_copy(out=xbf[:, tt, :], in_=xres[:, tt, :]),
               reads=[bf(f"xres{tt}")], writes=[bf(f"xbf{tt}")])
        emit_xT(0, 0, ringA)
        build_table(0, 0)
        dma(POOL, dsem("w_out"), w_out_bf[:], w_out.rearrange("(kc p) c -> p kc c", p=128), writes=[bf("w_out")])
        prep_weights()

        def ln_c2_norm(si, tg, tt):
            xs = tt
            op(DVE, lambda: nc.vector.scalar_tensor_tensor(out=xres[:, xs, :], in0=xres[:, xs, :], scalar=mv[:, xs, 3:4], in1=lng_bc[:],
                                                           op0=ALU.add, op1=ALU.mult),
               reads=[bf(f"xres{xs}"), bf(f"mvc{xs}"), bf("lnc")], writes=[bf(f"xres{xs}")])

        def ln_c2_gb(si, tg, tt):
            t0 = tg * TG + tt * 128
            xs = tt
            op(DVE, lambda: nc.vector.scalar_tensor_tensor(out=xres[:, xs, :], in0=xres[:, xs, :], scalar=mv[:, xs, 2:3], in1=lnb_bc[:],
                                                           op0=ALU.mult, op1=ALU.add),
               reads=[bf(f"xres{xs}"), bf(f"mvb{xs}"), bf("lnc")], writes=[bf(f"xres{xs}")])
            dma(SP, st_sem[xs], out[si, t0:t0 + 128, :], xres[:, xs, :], reads=[bf(f"xres{xs}")])

        def ln_c2(si, tg, tt):
            ln_c2_norm(si, tg, tt)
            ln_c2_gb(si, tg, tt)

        tail_pending = []
        for si in range(nseq):
            for tg in range(NTG):
                tok0 = tg * TG
                nsi, ntg = (si, tg + 1) if tg + 1 < NTG else (si + 1, 0)
                last_tg = (si == nseq - 1 and tg == NTG - 1)

                def inproj_chunk(c0):
                    bi = next_bank(ringA)
                    for kc in range(8):
                        op(PE, lambda kc=kc, bi=bi: nc.tensor.matmul(banks[bi][:], lhsT=w_in_bf[:, kc, c0:c0 + 128], rhs=xT[:, kc, :],
                                                                     start=(kc == 0), stop=(kc == 7)),
                           reads=XT_ALL + SEG[c0], writes=[bbuf[bi]], inc=(kc == 7))
                    return bi

                for g in range(4):
                    w = POOL_W[g]
                    bi = inproj_chunk(C_U + g * 128)
                    op(ACT, lambda bi=bi, g=g: nc.scalar.activation(out=ubuf[:, g, HALO:HALO + TG], in_=banks[bi][:], func=AF.Copy),
                       reads=[bbuf[bi]], writes=[bf(f"u{g}")])
                    L = HALO + TG
                    src = ubuf[:, g, :]
                    nsteps = g + 1
                    sh = 1
                    lo = HALO - (w - 1)
                    cur_buf = bf(f"u{g}")
                    for st in range(nsteps):
                        dst = pws[:, st % 2, :]
                        dbuf = bf(f"pws{st % 2}")
                        lo2 = lo + sh
                        op(DVE, lambda dst=dst, src=src, lo2=lo2, sh=sh, L=L: nc.vector.tensor_tensor(
                            out=dst[:, lo2:L], in0=src[:, lo2:L], in1=src[:, lo2 - sh:L - sh], op=ALU.add),
                           reads=[cur_buf], writes=[dbuf])
                        src, cur_buf, lo, sh = dst, dbuf, lo2, sh * 2
                    if tg == 0:
                        op(DVE, lambda src=src, g=g: nc.vector.tensor_tensor(out=src[:, HALO:2 * HALO], in0=src[:, HALO:2 * HALO],
                                                                              in1=poolc[:, g, :], op=ALU.mult),
                           reads=[cur_buf, bf("consts")], writes=[cur_buf])
                    op(DVE, lambda g=g: nc.vector.tensor_copy(out=ubuf[:, g, 0:HALO], in_=ubuf[:, g, TG:TG + HALO]),
                       reads=[bf(f"u{g}")], writes=[bf(f"u{g}")])
                    op(DVE, lambda src=src, g=g, w=w: nc.vector.scalar_tensor_tensor(out=pooled_on[:, g, :], in0=ubuf[:, g, HALO:HALO + TG],
                                                                                    scalar=-float(w), in1=src[:, HALO:HALO + TG],
                                                                                    op0=ALU.mult, op1=ALU.add),
                       reads=[cur_buf, bf(f"u{g}")], writes=[bf(f"po{g}")])
                if tg == NTG - 1:
                    op(DVE, lambda: nc.vector.memset(ubuf[:, :, 0:HALO], 0.0), writes=[bf(f"u{g}") for g in range(4)])

                while tail_pending:
                    tail_pending.pop(0)()
                for c in range(4):
                    bi = inproj_chunk(C_GA + c * 128)
                    op(ACT, lambda bi=bi, c=c: nc.scalar.activation(out=sga[:, c, :], in_=banks[bi][:], func=AF.Silu),
                       reads=[bbuf[bi]], writes=[bf(f"sga{c}")])
                for c in range(4):
                    bi = inproj_chunk(C_GB + c * 128)
                    op(ACT, lambda bi=bi, c=c: nc.scalar.activation(out=sgb[:, c, :], in_=banks[bi][:], func=AF.Silu),
                       reads=[bbuf[bi]], writes=[bf(f"sgb{c}")])
                table_sin()
                op(ACT, lambda: nc.scalar.activation(out=smalls[:, 5:6], in_=smalls[:, 0:1], func=AF.Ln),
                   reads=[bf("smalls")], writes=[bf("dummy")])
                ssq_bank = {"q": 2, "kv": 3}
                SQ = [(sq_bf[:, 0, :], bf("sq0")), (sq_bf[:, 1, :], bf("sq1"))] + [(pT[:, i, :], bf(f"pT{i}")) for i in range(4)]
                colbank = 4
                sqi = 0
                pending = []

                def flush_pending():
                    while pending:
                        pending.pop(0)()

                def ssq_q_mm(sl, c):
                    op(PE, lambda: nc.tensor.matmul(banks[ssq_bank["q"]][:], lhsT=ones, rhs=SQ[sl][0],
                                                    start=(c == 0), stop=(c == 3)),
                       reads=[SQ[sl][1], bf("mats")], writes=[bbuf[ssq_bank["q"]]], inc=True)

                def ssq_kv_mm(sl, c):
                    op(PE, lambda: nc.tensor.matmul(banks[ssq_bank["kv"]][:], lhsT=ones, rhs=SQ[sl][0],
                                                    start=(c == 0), stop=(c == 1)),
                       reads=[SQ[sl][1], bf("mats")], writes=[bbuf[ssq_bank["kv"]]], inc=False)
                    for tt in range(4):
                        op(PE, lambda tt=tt: nc.tensor.matmul(banks[colbank][:, tt:tt + 1], lhsT=SQ[sl][0][:, tt * 128:(tt + 1) * 128],
                                                              rhs=ones[:, 0:1], start=(c == 0 and tt == 0), stop=(c == 1),
                                                              skip_group_check=True),
                           reads=[SQ[sl][1], bf("mats")], writes=[bbuf[colbank]], inc=(tt == 3))

                for c in range(4):
                    bi = inproj_chunk(C_XQ + c * 128)
                    flush_pending()
                    op(ACT, lambda bi=bi, c=c: nc.scalar.activation(out=xq_bf[:, c, :], in_=banks[bi][:], func=AF.Copy),
                       reads=[bbuf[bi]], writes=[bf(f"xq{c}")])
                    sl = sqi % 6
                    sqi += 1
                    op(ACT, lambda bi=bi, sl=sl: nc.scalar.activation(out=SQ[sl][0], in_=banks[bi][:], func=AF.Square),
                       reads=[bbuf[bi]], writes=[SQ[sl][1]])
                    pending.append(lambda sl=sl, c=c: ssq_q_mm(sl, c))
                for c in range(2):
                    bi = inproj_chunk(C_XKV + c * 128)
                    flush_pending()
                    op(ACT, lambda bi=bi, c=c: nc.scalar.activation(out=xkv_bf[:, c, :], in_=banks[bi][:], func=AF.Copy),
                       reads=[bbuf[bi]], writes=[bf(f"xkv{c}")])
                    sl = sqi % 6
                    sqi += 1
                    op(ACT, lambda bi=bi, sl=sl: nc.scalar.activation(out=SQ[sl][0], in_=banks[bi][:], func=AF.Square),
                       reads=[bbuf[bi]], writes=[SQ[sl][1]])
                    pending.append(lambda sl=sl, c=c: ssq_kv_mm(sl, c))
                for g in range(4):
                    bi = next_bank(ringA)
                    op(PE, lambda g=g, bi=bi: nc.tensor.matmul(banks[bi][:], lhsT=pool_w_bf[:, g, :], rhs=pooled_on[:, g, :], start=True, stop=True),
                       reads=[bf(f"po{g}"), bf("pool_w")], writes=[bbuf[bi]])
                    op(DVE, lambda g=g, bi=bi: nc.vector.scalar_tensor_tensor(out=yT[:, 4 + g, :], in0=banks[bi][:], scalar=pscale[:, g:g + 1],
                                                                            in1=sgb[:, g, :], op0=ALU.mult, op1=ALU.mult),
                       reads=[bbuf[bi], bf(f"sgb{g}"), bf("constg")], writes=[bf(f"yT{4 + g}")])

                bi_kr = inproj_chunk(C_KR)
                flush_pending()
                op(DVE, lambda bi=bi_kr: nc.vector.tensor_tensor(out=pws[:, 0, 0:TG], in0=banks[bi][:], in1=tabT[:], op=ALU.mult),
                   reads=[bbuf[bi_kr], bf("tabT")], writes=[bf("pws0")])
                op(DVE, lambda: nc.vector.tensor_copy(out=tabu[0:64, :], in_=pws[64:128, 0, 0:TG]), reads=[bf("pws0")], writes=[bf("tabu")])
                op(DVE, lambda: nc.vector.tensor_tensor(out=kdup[0:64, tok0:tok0 + TG], in0=pws[0:64, 0, 0:TG], in1=tabu[0:64, :], op=ALU.add),
                   reads=[bf("pws0"), bf("tabu")], writes=[bf("kdup"), bf("kdstage")])
                op(DVE, lambda: nc.vector.tensor_copy(out=kdup[64:128, tok0:tok0 + TG], in_=kdup[0:64, tok0:tok0 + TG]),
                   reads=[bf("kdup")], writes=[bf("kdup")])

                op(ACT, lambda: nc.scalar.activation(out=rinvq[:], in_=banks[ssq_bank["q"]][:], func=AF.Ln, bias=eps_rms, scale=1.0 / 512),
                   reads=[bbuf[ssq_bank["q"]], bf("smalls")], writes=[bf("rinvq")])
                op(ACT, lambda: nc.scalar.activation(out=rinvq[:], in_=rinvq[:], func=AF.Exp, scale=-0.5),
                   reads=[bf("rinvq")], writes=[bf("rinvq")])
                op(ACT, lambda: nc.scalar.activation(out=rinvkv[:], in_=banks[ssq_bank["kv"]][:], func=AF.Ln, bias=eps_rms, scale=1.0 / 256),
                   reads=[bbuf[ssq_bank["kv"]], bf("smalls")], writes=[bf("rinvkv")])
                op(ACT, lambda: nc.scalar.activation(out=rinvkv[:], in_=rinvkv[:], func=AF.Exp, scale=-0.5),
                   reads=[bf("rinvkv")], writes=[bf("rinvkv")])
                op(ACT, lambda: nc.scalar.activation(out=rinvcol[:], in_=banks[colbank][:, 0:4], func=AF.Ln, bias=eps_rms, scale=1.0 / 256),
                   reads=[bbuf[colbank], bf("smalls")], writes=[bf("rinvcol")])
                op(ACT, lambda: nc.scalar.activation(out=rinvcol[:], in_=rinvcol[:], func=AF.Exp, scale=-0.5),
                   reads=[bf("rinvcol")], writes=[bf("rinvcol")])
                op(DVE, lambda: nc.vector.tensor_tensor(out=tabT[:], in0=tabT[:], in1=rinvq[:], op=ALU.mult),
                   reads=[bf("tabT"), bf("rinvq")], writes=[bf("tabT")])

                XQ = [bf(f"xq{c}") for c in range(4)]
                XKV = [bf(f"xkv{c}") for c in range(2)]
                for h in range(4):
                    for part in range(2):
                        bi = next_bank(ringA)
                        c0 = 0 if part == 0 else 128
                        for kc in range(4):
                            op(PE, lambda kc=kc, bi=bi, h=h, c0=c0: nc.tensor.matmul(banks[bi][:], lhsT=w_uq_bf[:, kc, h, c0:c0 + 128], rhs=xq_bf[:, kc, :],
                                                                                    start=(kc == 0), stop=(kc == 3)),
                               reads=XQ + [bf("w_uq")], writes=[bbuf[bi]], inc=(kc == 3))
                        if part == 0:
                            op(DVE, lambda bi=bi, h=h: nc.vector.tensor_tensor(out=qn[:, h, :], in0=banks[bi][:], in1=rinvq[:], op=ALU.mult),
                               reads=[bbuf[bi], bf("rinvq")], writes=[bf(f"qn{h}")])
                        else:
                            op(DVE, lambda bi=bi, h=h: nc.vector.tensor_tensor(out=qr[:, h, :], in0=banks[bi][:], in1=tabT[:], op=ALU.mult),
                               reads=[bbuf[bi], bf("tabT")], writes=[bf(f"qr{h}")])
                for h in range(4):
                    bi = next_bank(ringA)
                    for kc in range(2):
                        op(PE, lambda kc=kc, bi=bi, h=h: nc.tensor.matmul(banks[bi][:], lhsT=w_ukvk_bf[:, kc, h, :], rhs=xkv_bf[:, kc, :],
                                                                         start=(kc == 0), stop=(kc == 1)),
                           reads=XKV + [bf("w_ukv")], writes=[bbuf[bi]], inc=(kc == 1))
                    op(DVE, lambda bi=bi, h=h: nc.vector.tensor_tensor(out=Kn[:, h, tok0:tok0 + TG], in0=banks[bi][:], in1=rinvkv[:], op=ALU.mult),
                       reads=[bbuf[bi], bf("rinvkv")], writes=[bf(f"Kn{h}"), bf("Kstage")])
                for tt in range(4):
                    bi = next_bank(ringA)
                    kb = tg * 4 + tt
                    for kc in range(2):
                        op(PE, lambda kc=kc, bi=bi, tt=tt: nc.tensor.matmul(banks[bi][:], lhsT=xkv_bf[:, kc, tt * 128:(tt + 1) * 128], rhs=w_ukvv_bf[:, kc, :],
                                                                           start=(kc == 0), stop=(kc == 1)),
                           reads=XKV + [bf("w_ukv")], writes=[bbuf[bi]], inc=(kc == 1))
                    if tt < 2:
                        op(ACT, lambda bi=bi, tt=tt, kb=kb: nc.scalar.activation(out=Vaug[:, kb, :, 0:128],
                                                                                in_=banks[bi][:].rearrange("p (h c) -> p h c", h=4),
                                                                                func=AF.Identity, scale=rinvcol[:, tt:tt + 1]),
                           reads=[bbuf[bi], bf("rinvcol")], writes=[bf("V"), bf("Vstage")])
                    else:
                        op(DVE, lambda bi=bi, tt=tt, kb=kb: nc.vector.tensor_scalar(out=Vaug[:, kb, :, 0:128],
                                                                                   in0=banks[bi][:].rearrange("p (h c) -> p h c", h=4),
                                                                                   scalar1=rinvcol[:, tt:tt + 1], scalar2=None, op0=ALU.mult),
                           reads=[bbuf[bi], bf("rinvcol")], writes=[bf("V"), bf("Vstage")])

                for tt in range(4):
                    t0 = tok0 + tt * 128
                    dma(SP, xr_sem[tt], xres[:, tt, :], x[si, t0:t0 + 128, :], writes=[bf(f"xres{tt}")])
                if nsi < nseq:
                    build_table(nsi, ntg)
                nkb = 4 * (tg + 1)
                steps = []

                def emit_pv(h, kb, slot, part=None):
                    pvset = (4, 5) if h % 2 == 0 else (6, 7)
                    j = kb - 4 * tg
                    qls = list(range(max(j, 0), 4))
                    if part == 0:
                        qls = qls[:len(qls) // 2]
                    elif part == 1:
                        qls = qls[len(qls) // 2:]
                    for ql in qls:
                        pb = pvset[ql // 2]
                        gcol = (ql % 2) * 129
                        op(PE, lambda ql=ql, pb=pb, gcol=gcol: nc.tensor.matmul(
                            banks[pb][:, gcol:gcol + 129], lhsT=pT[:, slot, ql * 128:(ql + 1) * 128], rhs=Vaug[:, kb, h, :],
                            start=(kb == 0 and ql % 2 == 0), stop=(kb == 4 * tg + ql), skip_group_check=True),
                           reads=[bf(f"pT{slot}"), bf("V"), bf("Vones")], writes=[bbuf[pb]], inc=(ql == 3))
                    if j in (1, 3) and part != 0:
                        half = j // 2
                        pb = pvset[half]
                        pv3 = banks[pb][:, 0:258].rearrange("p (g c) -> p g c", g=2)
                        op(DVE, lambda: nc.vector.reciprocal(out=rs[:, h, :], in_=pv3[:, :, 128]),
                           reads=[bbuf[pb]], writes=[bf(f"rs{h}")])
                        op(DVE, lambda: nc.vector.tensor_tensor(
                            out=pooled_on[:, 2 * half:2 * half + 2, h * 128:(h + 1) * 128], in0=pv3[:, :, 0:128],
                            in1=rs[:, h, :].unsqueeze(2).to_broadcast([128, 2, 128]), op=ALU.mult),
                           reads=[bbuf[pb], bf(f"rs{h}")], writes=[bf(f"po{2 * half}"), bf(f"po{2 * half + 1}")])

                for h in range(4):
                    for kb in range(nkb):
                        j = kb - 4 * tg
                        c0 = 128 * j if j > 0 else 0
                        bi = next_bank(ringS)
                        slot = state["pt"] % 4
                        state["pt"] += 1
                        op(PE, lambda bi=bi, kb=kb, c0=c0, h=h: nc.tensor.matmul(banks[bi][:, c0:TG], lhsT=Kn[:, h, kb * 128:(kb + 1) * 128], rhs=qn[:, h, c0:TG],
                                                                                start=True, stop=False),
                           reads=[bf(f"Kn{h}"), bf(f"qn{h}")], writes=[bbuf[bi]], inc=False)
                        st = steps.pop(0) if len(steps) >= 3 else None
                        if st is not None:
                            emit_pv(*st, part=0)
                        op(PE, lambda bi=bi, kb=kb, c0=c0, h=h, j=j: nc.tensor.matmul(banks[bi][:, c0:TG], lhsT=kdup[:, kb * 128:(kb + 1) * 128], rhs=qr[:, h, c0:TG],
                                                                                     start=False, stop=(j < 0)),
                           reads=[bf("kdup"), bf(f"qr{h}")], writes=[bbuf[bi]], inc=(j < 0))
                        if j >= 0:
                            op(PE, lambda bi=bi, c0=c0: nc.tensor.matmul(banks[bi][:, c0:c0 + 128], lhsT=ident, rhs=negmask, start=False, stop=True),
                               reads=[bf("mats")], writes=[bbuf[bi]], inc=True)
                        op(ACT, lambda bi=bi, slot=slot, c0=c0: nc.scalar.activation(out=pT[:, slot, c0:TG], in_=banks[bi][:, c0:TG], func=AF.Exp,
                                                                                    bias=zero_col, scale=SCALE),
                           reads=[bbuf[bi], bf("smalls")], writes=[bf(f"pT{slot}")])
                        if st is not None:
                            emit_pv(*st, part=1)
                        steps.append((h, kb, slot))
                while steps:
                    emit_pv(*steps.pop(0))
                if nsi < nseq:
                    state["bank"] = 0
                    emit_xT(nsi, ntg, [4, 5, 0, 1, 2, 3])

                for tt in range(4):
                    bi = next_bank(ringC)
                    pv = banks[bi][:].bitcast(BF16)[:, 0:512].rearrange("p (a b) -> p a b", a=4)
                    for h in range(4):
                        op(PE, lambda pv=pv, h=h, tt=tt: nc.tensor.transpose(out=pv[:, h, :], in_=pooled_on[:, tt, h * 128:(h + 1) * 128], identity=ident),
                           reads=[bf(f"po{tt}"), bf("mats")], writes=[bbuf[bi]], inc=(h == 3))
                    op(DVE, lambda pv=pv, tt=tt: nc.vector.tensor_tensor(out=yT[:, 0:4, tt * 128:(tt + 1) * 128], in0=pv,
                                                                        in1=sga[:, :, tt * 128:(tt + 1) * 128], op=ALU.mult),
                       reads=[bbuf[bi]] + [bf(f"sga{c}") for c in range(4)], writes=[bf(f"yTa{tt}")])
                for tt in range(4):
                    t0 = tok0 + tt * 128
                    xs = tt
                    if last_tg and tt > 0:
                        ln_c2_norm(si, tg, tt - 1)
                    YDEPS = [bf(f"yTa{tt}")] + [bf(f"yT{4 + g}") for g in range(4)] + [bf("w_out")]
                    for half in range(2):
                        bi = next_bank(ringC)
                        for c in range(8):
                            op(PE, lambda c=c, bi=bi, half=half, tt=tt: nc.tensor.matmul(banks[bi][:], lhsT=yT[:, c, tt * 128:(tt + 1) * 128],
                                                                                        rhs=w_out_bf[:, c, half * 512:(half + 1) * 512],
                                                                                        start=(c == 0), stop=(c == 7)),
                               reads=YDEPS, writes=[bbuf[bi]], inc=(c == 7))
                        op(DVE, lambda bi=bi, half=half, xs=xs: nc.vector.scalar_tensor_tensor(
                            out=xres[:, xs, half * 512:(half + 1) * 512], in0=xres[:, xs, half * 512:(half + 1) * 512], scalar=ALPHA,
                            in1=banks[bi][:], op0=ALU.mult, op1=ALU.add),
                           reads=[bbuf[bi], bf(f"xres{xs}")], writes=[bf(f"xres{xs}")])
                        op(DVE, lambda half=half, xs=xs: nc.vector.bn_stats(out=stats[:, xs, half * 6:(half + 1) * 6],
                                                                           in_=xres[:, xs, half * 512:(half + 1) * 512]),
                           reads=[bf(f"xres{xs}")], writes=[bf(f"stats{xs}")])
                    op(DVE, lambda xs=xs: nc.vector.bn_aggr(out=mv[:, xs, 0:2], in_=stats[:, xs, :]),
                       reads=[bf(f"stats{xs}")], writes=[bf(f"mv{xs}")])
                    op(DVE, lambda xs=xs: nc.vector.tensor_scalar(out=mv[:, xs, 3:4], in0=mv[:, xs, 0:1], scalar1=-1.0, scalar2=None, op0=ALU.mult),
                       reads=[bf(f"mv{xs}")], writes=[bf(f"mvc{xs}")])
                    op(ACT, lambda xs=xs: nc.scalar.activation(out=mv[:, xs, 2:3], in_=mv[:, xs, 1:2], func=AF.Ln, bias=eps_ln, scale=1.0),
                       reads=[bf(f"mv{xs}"), bf("smalls")], writes=[bf(f"mvb{xs}")])
                    op(ACT, lambda xs=xs: nc.scalar.activation(out=mv[:, xs, 2:3], in_=mv[:, xs, 2:3], func=AF.Exp, scale=-0.5),
                       reads=[bf(f"mvb{xs}")], writes=[bf(f"mvb{xs}")])
                    if last_tg and tt > 0:
                        ln_c2_gb(si, tg, tt - 1)
                if last_tg:
                    ln_c2(si, tg, 3)
                else:
                    op(ACT, lambda: nc.scalar.activation(out=smalls[:, 4:5], in_=smalls[:, 2:3], func=AF.Silu),
                       reads=[bf("smalls")], writes=[bf("dummy")])
                    for tt in range(4):
                        ln_c2(si, tg, tt)

        while tail_pending:
            tail_pending.pop(0)()
        for xs in range(4):
            nc.sync.wait_ge(st_sem[xs].sem, st_sem[xs].n)
    return nc


def _consts():
    half = 32
    inv_freq = (10000.0 ** (-np.arange(half, dtype=np.float64) / half)) / (2.0 * np.pi)
    cols = np.zeros((128, 8), np.float32)
    for p in range(128):
        hi = np.float32(inv_freq[p % 32])
        cols[p, 0] = hi
        cols[p, 2] = np.float32(inv_freq[p % 32] - np.float64(hi))
        cols[p, 1] = 0.25 if p < 64 else 0.0
    pc = np.ones((128, 4, HALO), np.float32)
    for g, w in enumerate(POOL_W):
        for t in range(HALO):
            pc[:, g, t] = w / min(t + 1, w)
    mats = np.zeros((128, 3, 128), np.float32)
    mats[:, 0, :] = np.eye(128, dtype=np.float32)
    k = np.arange(128)[:, None]
    q = np.arange(128)[None, :]
    mats[:, 1, :] = np.where(k > q, NEG, 0.0)
    mats[:, 2, :] = 1.0
    return cols, pc.reshape(128, 4 * HALO), mats


_NC_CACHE = {}


def run(inputs, nseq, ncores):
    if nseq not in _NC_CACHE:
        _NC_CACHE[nseq] = build_nc(nseq)
    nc = _NC_CACHE[nseq]
    cols, pc, mats = _consts()
    x = np.ascontiguousarray(np.asarray(inputs["x"], dtype=np.float32))
    pos = np.ascontiguousarray(np.asarray(inputs["positions"], dtype=np.int32))
    shared = {k: np.ascontiguousarray(np.asarray(inputs[k], dtype=np.float32)) for k in
              ("w_in", "q_norm_g", "w_uq", "kv_norm_g", "w_ukv", "pool_w", "pool_scale", "w_out", "ln_g", "ln_b")}
    shared.update({"cst_cols": cols, "pool_c": pc, "cst_mats": mats})
    in_maps = []
    for i in range(ncores):
        m = dict(shared)
        m["x"] = x[i * nseq:(i + 1) * nseq]
        m["positions"] = pos[i * nseq:(i + 1) * nseq]
        in_maps.append(m)
    res = run_bass_kernel_spmd(nc, in_maps, core_ids=list(range(ncores)))
    return np.concatenate([r["out"] for r in res.results], axis=0)


def kernel(x, positions, w_in, q_norm_g, w_uq, kv_norm_g, w_ukv, pool_w, pool_scale, w_out, ln_g, ln_b):
    inputs = dict(x=x, positions=positions, w_in=w_in, q_norm_g=q_norm_g, w_uq=w_uq, kv_norm_g=kv_norm_g, w_ukv=w_ukv,
                  pool_w=pool_w, pool_scale=pool_scale, w_out=w_out, ln_g=ln_g, ln_b=ln_b)
    return run(inputs, NSEQ, NCORES).astype(np.float32)
```
